# Optimizing a Trainium2 kernel written in Bass

```python
import math
import jax, jax.numpy as jnp
from jax import lax
import numpy as np

D_MODEL = 1024
BATCH = 8
SEQ = 4096
DEPTH = 1

D_PLE = 256
D_MIX = D_MODEL
ATTN_HEADS = 8
HEAD_DIM = 64
D_ATTN = ATTN_HEADS * HEAD_DIM
POOL_WINDOWS = (2, 4, 8, 16)
POOL_GROUPS = len(POOL_WINDOWS)
D_POOL = D_MIX - D_ATTN
POOL_CH = D_POOL // POOL_GROUPS
D_IN = 3 * D_ATTN + ATTN_HEADS + D_POOL
D_FF = int(math.ceil(8 * D_MODEL / 3 / 256) * 256)
Q_BLOCK = 128
RMS_EPS = 1e-6

kernel_name = "hymba_fox_poolformer_block"


def rms_norm(x, g):
    xf = x.astype(jnp.float32)
    y = xf * lax.rsqrt(jnp.mean(xf * xf, axis=-1, keepdims=True) + RMS_EPS)
    return (y * g.astype(jnp.float32)).astype(x.dtype)


def forgetting_attention(q, k, v, log_f):
    b, s, h, dh = q.shape
    scale = 1.0 / math.sqrt(dh)
    c = jnp.cumsum(log_f, axis=1).transpose(0, 2, 1)
    nb = s // Q_BLOCK
    qb = q.reshape(b, nb, Q_BLOCK, h, dh).transpose(1, 0, 2, 3, 4)
    cb = c.reshape(b, h, nb, Q_BLOCK).transpose(2, 0, 1, 3)
    starts = jnp.arange(nb, dtype=jnp.int32) * Q_BLOCK
    kpos = jnp.arange(s, dtype=jnp.int32)

    def one_block(args):
        q_i, c_i, s0 = args
        scores = jnp.einsum('bqhd,bkhd->bhqk', q_i, k).astype(jnp.float32) * scale
        scores = scores + c_i[:, :, :, None] - c[:, :, None, :]
        qpos = s0 + jnp.arange(Q_BLOCK, dtype=jnp.int32)
        causal = kpos[None, :] <= qpos[:, None]
        scores = jnp.where(causal, scores, -jnp.inf)
        probs = jax.nn.softmax(scores, axis=-1).astype(v.dtype)
        return jnp.einsum('bhqk,bkhd->bqhd', probs, v)

    out = lax.map(one_block, (qb, cb, starts))
    return out.transpose(1, 0, 2, 3, 4).reshape(b, s, h * dh)


def multiscale_pool(u, w_pool, pool_scale):
    b, s, _ = u.shape
    uf = u.astype(jnp.float32)
    cs = jnp.cumsum(uf, axis=1)
    pos = jnp.arange(s, dtype=jnp.int32)
    outs = []
    for g, w in enumerate(POOL_WINDOWS):
        lo, hi = g * POOL_CH, (g + 1) * POOL_CH
        cs_g = cs[:, :, lo:hi]
        cs_shift = jnp.pad(cs_g, ((0, 0), (w, 0), (0, 0)))[:, :s]
        count = jnp.minimum(pos + 1, w).astype(jnp.float32)[None, :, None]
        outs.append((cs_g - cs_shift) / count - uf[:, :, lo:hi])
    y = jnp.stack(outs, axis=2).astype(u.dtype)
    y = jnp.einsum('bsgc,gcd->bsgd', y, w_pool).reshape(b, s, D_POOL)
    return y * pool_scale


def setup_inputs(seed: int = 0) -> dict:
    key = jax.random.key(seed)
    ks = jax.random.split(key, 24)
    f32 = jnp.float32

    def nrm(k, shape, fan_in):
        return jax.random.normal(k, shape, f32) * (fan_in ** -0.5)

    def gain(k, shape):
        return 1.0 + 0.05 * jax.random.normal(k, shape, f32)

    return {
        "x": jax.random.normal(ks[0], (BATCH, SEQ, D_MODEL), f32),
        "p": jax.random.normal(ks[1], (DEPTH, BATCH, SEQ, D_PLE), f32),
        "g_mix_pre": gain(ks[2], (DEPTH, D_MODEL)),
        "w_in": nrm(ks[3], (DEPTH, D_MODEL, D_IN), D_MODEL),
        "b_forget": jax.random.uniform(ks[4], (DEPTH, ATTN_HEADS), f32, 1.0, 4.0),
        "g_attn_grp": gain(ks[5], (DEPTH, D_ATTN)),
        "g_pool_grp": gain(ks[6], (DEPTH, D_POOL)),
        "w_pool": nrm(ks[7], (DEPTH, POOL_GROUPS, POOL_CH, POOL_CH), POOL_CH),
        "pool_scale": 1.0 + 0.1 * jax.random.normal(ks[8], (DEPTH, D_POOL), f32),
        "w_out": nrm(ks[9], (DEPTH, D_MIX, D_MODEL), D_MIX),
        "g_mix_post": gain(ks[10], (DEPTH, D_MODEL)),
        "g_ffn_pre": gain(ks[11], (DEPTH, D_MODEL)),
        "w_ffn_gate": nrm(ks[12], (DEPTH, D_MODEL, D_FF), D_MODEL),
        "w_ffn_up": nrm(ks[13], (DEPTH, D_MODEL, D_FF), D_MODEL),
        "w_ffn_down": nrm(ks[14], (DEPTH, D_FF, D_MODEL), D_FF),
        "g_ffn_post": gain(ks[15], (DEPTH, D_MODEL)),
        "w_ple_proj": nrm(ks[16], (DEPTH, D_PLE, D_MODEL), D_PLE),
        "g_ple": gain(ks[17], (DEPTH, D_MODEL)),
        "w_ple_gate": nrm(ks[18], (DEPTH, D_MODEL, D_MODEL), D_MODEL),
    }


def reference(x, p, g_mix_pre, w_in, b_forget, g_attn_grp, g_pool_grp, w_pool, pool_scale,
              w_out, g_mix_post, g_ffn_pre, w_ffn_gate, w_ffn_up, w_ffn_down, g_ffn_post,
              w_ple_proj, g_ple, w_ple_gate):
    b, s, _ = x.shape
    h = x
    for i in range(DEPTH):
        hn = rms_norm(h, g_mix_pre[i])
        z = hn @ w_in[i]
        o = 0
        q = z[..., o:o + D_ATTN].reshape(b, s, ATTN_HEADS, HEAD_DIM); o += D_ATTN
        k = z[..., o:o + D_ATTN].reshape(b, s, ATTN_HEADS, HEAD_DIM); o += D_ATTN
        v = z[..., o:o + D_ATTN].reshape(b, s, ATTN_HEADS, HEAD_DIM); o += D_ATTN
        f_logit = z[..., o:o + ATTN_HEADS]; o += ATTN_HEADS
        u = z[..., o:o + D_POOL]
        log_f = jax.nn.log_sigmoid(f_logit.astype(jnp.float32) + b_forget[i].astype(jnp.float32))
        a = forgetting_attention(q, k, v, log_f)
        m = multiscale_pool(u, w_pool[i], pool_scale[i])
        mix = jnp.concatenate([rms_norm(a, g_attn_grp[i]), rms_norm(m, g_pool_grp[i])], axis=-1)
        h = h + rms_norm(mix @ w_out[i], g_mix_post[i])
        hn = rms_norm(h, g_ffn_pre[i])
        ff = (jax.nn.silu(hn @ w_ffn_gate[i]) * (hn @ w_ffn_up[i])) @ w_ffn_down[i]
        h = h + rms_norm(ff, g_ffn_post[i])
        e = rms_norm(p[i] @ w_ple_proj[i], g_ple[i])
        h = h + jax.nn.sigmoid(h @ w_ple_gate[i]) * e
    return h
```

```python
import numpy as np
import concourse.bass as bass
import concourse.mybir as mybir
from concourse.bass_utils import run_bass_kernel_spmd

F32 = mybir.dt.float32
BF16 = mybir.dt.bfloat16
AF = mybir.ActivationFunctionType
ALU = mybir.AluOpType

D = 1024
DPLE = 256
NH = 8
HD = 64
DATT = 512
DPOOL = 512
DIN = 2056
DFF = 2816
NFC = DFF // 128
EPS = 1e-6
TB = 512
WINS = (2, 4, 8, 16)


class Buf:
    __slots__ = ("name", "w", "rs")

    def __init__(self, name):
        self.name = name
        self.w = None
        self.rs = []


class Op:
    __slots__ = ("eng", "fn", "deps", "needs_inc", "is_dma", "semkey", "sem", "semval", "grp")


class Sched:
    ENGS = ["pe", "act", "dve", "pool", "sp"]

    def __init__(self, nc):
        self.nc = nc
        self.ops = {e: [] for e in self.ENGS}
        self.dma_cum = {}

    def op(self, eng, fn, reads=(), writes=(), dma=False, semkey=None, grp=None):
        o = Op()
        o.eng = eng
        o.fn = fn
        o.is_dma = dma
        o.needs_inc = dma
        o.semkey = semkey
        o.grp = grp
        o.sem = None
        o.semval = None
        deps = []

        def add(d, raw):
            if d is o:
                return
            if grp is not None and d.is_dma and d.grp == grp:
                return
            if (not d.is_dma) and (not dma) and d.eng == eng and eng == "pe":
                return
            if d not in deps:
                deps.append(d)

        for b in reads:
            if b.w is not None:
                add(b.w, True)
        for b in writes:
            if b.w is not None:
                add(b.w, False)
            for r in b.rs:
                add(r, False)
        for b in reads:
            b.rs.append(o)
        for b in writes:
            b.w = o
            b.rs = []
        o.deps = deps
        for d in deps:
            d.needs_inc = True
        if dma:
            self.dma_cum[semkey] = self.dma_cum.get(semkey, 0) + 16
            o.semval = self.dma_cum[semkey]
        self.ops[eng].append(o)
        return o

    def emit(self):
        nc = self.nc
        esem = {}
        for e in ["pe", "act", "dve", "pool"]:
            esem[e] = nc.alloc_semaphore("prog_" + e)
            cnt = 0
            for o in self.ops[e]:
                if o.is_dma:
                    continue
                if o.needs_inc:
                    cnt += 1
                    o.sem = esem[e]
                    o.semval = cnt
        dsem = {k: nc.alloc_semaphore("dma_%s" % (k,)) for k in self.dma_cum}
        for e in self.ENGS:
            for o in self.ops[e]:
                if o.is_dma:
                    o.sem = dsem[o.semkey]
        final_waits = [(dsem[k], v) for k, v in self.dma_cum.items()]

        def run(ename, eng):
            waited = {}
            for o in self.ops[ename]:
                for d in o.deps:
                    key = d.sem.num
                    if waited.get(key, 0) >= d.semval:
                        continue
                    eng.wait_ge(d.sem, d.semval)
                    waited[key] = d.semval
                ins = o.fn(eng)
                if o.needs_inc:
                    ins.then_inc(o.sem, 16 if o.is_dma else 1)
            if ename == "sp":
                for s, v in final_waits:
                    if waited.get(s.num, 0) < v:
                        eng.wait_ge(s, v)

        with nc.Block() as block:
            @block.tensor
            def _(eng):
                run("pe", eng)

            @block.scalar
            def _(eng):
                run("act", eng)

            @block.vector
            def _(eng):
                run("dve", eng)

            @block.gpsimd
            def _(eng):
                run("pool", eng)

            @block.sync
            def _(eng):
                run("sp", eng)


NCST_MAT = 15
C_IDENT, C_TRI, C_E127, C_BT, C_BP, C_BT0 = 0, 1, 2, 3, 7, 11
SM_B, SM_PS, SM_GPRE, SM_GMIX, SM_GFFN = 0, 8, 520, 528, 536
NSMALL = 544


def host_consts():
    m = np.zeros((NCST_MAT, 128, 128), np.float32)
    i = np.arange(128)
    m[C_IDENT] = np.eye(128, dtype=np.float32)
    m[C_TRI] = (i[:, None] <= i[None, :]).astype(np.float32)
    m[C_E127][127, :] = 1.0
    for g, w in enumerate(WINS):
        s = i[:, None]
        t = i[None, :]
        band = ((t - s >= 0) & (t - s < w)).astype(np.float32)
        m[C_BT + g] = band / w - np.eye(128, dtype=np.float32)
        m[C_BP + g] = ((t + 128 - s) < w).astype(np.float32) / w
        cnt = np.minimum(i + 1, w).astype(np.float32)[None, :]
        m[C_BT0 + g] = band / cnt - np.eye(128, dtype=np.float32)
    return np.ascontiguousarray(m.transpose(1, 0, 2).reshape(128, NCST_MAT * 128))


def build_nc(S_LEN):
    NBLK = S_LEN // TB
    NT = S_LEN // 128
    nc = bass.Bass("TRN2", target_bir_lowering=False)
    dt_in = lambda n, s: nc.dram_tensor(n, s, F32, kind="ExternalInput").ap()
    x_d = dt_in("x", [S_LEN, D])
    p_d = dt_in("p", [S_LEN, DPLE])
    w_in_d = dt_in("w_in", [D, DIN])
    w_pool_d = dt_in("w_pool", [4, 128, 128])
    w_out_d = dt_in("w_out", [D, D])
    w_g_d = dt_in("w_ffn_gate", [D, DFF])
    w_u_d = dt_in("w_ffn_up", [D, DFF])
    w_d_d = dt_in("w_ffn_down", [DFF, D])
    w_ple_d = dt_in("w_ple_proj", [DPLE, D])
    w_gate_d = dt_in("w_ple_gate", [D, D])
    cst_d = dt_in("cst", [128, NCST_MAT * 128])
    small_d = dt_in("small", [128, NSMALL])
    gbc_d = dt_in("gbc", [128, 3 * D])
    y_d = nc.dram_tensor("y", [S_LEN, D], F32, kind="ExternalOutput").ap()

    scr = lambda n, s: nc.dram_tensor(n, s, BF16, kind="Internal").ap()
    win_b = scr("win_b", [D, DIN])
    wout_b = scr("wout_b", [D, D])
    wg_b = scr("wg_b", [D, DFF])
    wu_b = scr("wu_b", [D, DFF])
    wd_b = scr("wd_b", [DFF, D])
    wple_b = scr("wple_b", [DPLE, D])
    wgate_b = scr("wgate_b", [D, D])
    kt_d = scr("kt_d", [NH, 67, S_LEN])

    S = Sched(nc)
    sb = lambda n, s, d: nc.alloc_sbuf_tensor("s_" + n, s, d)

    Va = sb("Va", [128, NT, NH, 65], BF16)
    KTcur = sb("KTcur", [128, NH, TB], BF16)
    KTs = [sb("KTs%d" % i, [128, max(S_LEN - TB, 128)], BF16) for i in range(2)]
    QTa = sb("QTa", [128, NH, TB], BF16)
    xres = sb("xres", [128, 4, D], F32)
    actT = sb("actT", [128, NFC, TB], BF16)
    hT = actT[:, 0:8, :]
    A_sb = sb("A_sb", [128, 4, DATT], F32)
    gb = sb("gb", [128, D], F32)
    PT = [sb("PT%d" % i, [128, TB], BF16) for i in range(3)]
    pT = sb("pT", [128, 2, 128], BF16)
    u_sb = sb("u_sb", [128, 8, DPOOL], BF16)
    hbA = [sb("hb%d" % i, [128, D], BF16) for i in range(2)]
    hb2A = [sb("hb2_%d" % i, [128, D], BF16) for i in range(2)]
    hT2 = sb("hT2", [128, 8, TB], BF16)
    xs = [sb("xs%d" % i, [128, D], F32) for i in range(2)]
    t32A = [sb("t32_%d" % i, [128, D], F32) for i in range(2)]
    t32 = t32A[0]
    t32b = sb("t32b", [128, D], F32)
    sg32 = [t32b[:, 0:TB], t32b[:, TB:2 * TB]]
    yTsA = [sb("yTs%d" % i, [128, 512], BF16) for i in range(2)]
    ptile = sb("ptile", [128, 2, DPLE], F32)
    pb = sb("pb", [128, DPLE], BF16)
    RING_N = 4
    RING_E = 4224
    ring = [sb("ring%d" % i, [128, RING_E], BF16) for i in range(RING_N)]
    cbf = sb("cbf", [128, 14, 128], BF16)
    trif = sb("trif", [128, 128], F32)
    e127f = sb("e127f", [128, 128], F32)
    small = sb("small", [128, NSMALL], F32)
    wpool_bf = sb("wpool_bf", [128, 4, 128], BF16)
    negc = sb("negc", [128, NT, NH], F32)
    fbA = [sb("fb%d" % i, [128, NH], F32) for i in range(2)]
    e1A = [sb("e1%d" % i, [128, NH], F32) for i in range(2)]
    nlfA = [sb("nlf%d" % i, [128, NH], F32) for i in range(2)]
    r1A = [sb("r1%d" % i, [128, NH], F32) for i in range(2)]
    r2A = [sb("r2%d" % i, [128, NH], F32) for i in range(2)]
    cspA = [sb("csp%d" % i, [128, NH, 3], BF16) for i in range(2)]
    st_ms = sb("st_ms", [128, 8], F32)
    st_ln = sb("st_ln", [128, 8], F32)
    st_r = sb("st_r", [128, 8], F32)
    rl = sb("rl", [128, 4], F32)

    ps = nc.alloc_psum_tensor("ps", [128, 8, 512], F32)

    bk = [Buf("bank%d" % i) for i in range(8)]
    B_Va = [Buf("Va%d" % i) for i in range(NT)]
    B_KTcur = [Buf("KTcur%d" % h) for h in range(NH)]
    B_KTs = [Buf("KTs0"), Buf("KTs1")]
    B_ktd = [Buf("ktd%d" % i) for i in range(NBLK)]
    B_QTa = [Buf("QTa%d" % h) for h in range(NH)]
    B_x = [Buf("x%d" % s) for s in range(4)]
    B_hT = [Buf("hT%d" % s) for s in range(4)]
    B_hT2 = [Buf("hT2_%d" % s) for s in range(4)]
    B_xs = [Buf("xs0"), Buf("xs1")]
    B_pt = [Buf("pt0"), Buf("pt1")]
    B_actT = [Buf("actT%d" % c) for c in range(NFC)]
    B_A = [Buf("A%d" % s) for s in range(4)]
    B_PT = [Buf("PT%d" % i) for i in range(3)]
    B_u = [Buf("u%d" % i) for i in range(8)]
    B_ring = [Buf("ring%d" % i) for i in range(RING_N)]
    B_negc = [Buf("negc%d" % i) for i in range(NT)]
    B = {k: Buf(k) for k in ["gb", "pT", "hb0", "hb1", "hb2_0", "hb2_1", "t32_0", "t32_1", "t32b", "sg0", "sg1", "yTs0", "yTs1", "ptile", "pb", "cbf", "trif", "e127f",
                             "small", "wpool", "fb0", "fb1", "e10", "e11", "nlf0", "nlf1", "r10", "r11", "r20", "r21", "csp0", "csp1", "rl",
                             "win_b", "wout_b", "wg_b", "wu_b", "wd_b", "wple_b", "wgate_b"]}
    B_st = [Buf("st%d" % i) for i in range(8)]
    B_y = [Buf("y%d" % s) for s in range(4)]

    op = S.op

    def load_xp(I):
        r0 = I * TB
        op("sp", lambda e: e.dma_start(out=xres[:], in_=x_d[r0:r0 + TB, :].rearrange("(s p) d -> p s d", p=128)),
           writes=B_x, dma=True, semkey="x")

    stage = actT[:].rearrange("p c t -> p (c t)").bitcast(F32)
    op("sp", lambda e: e.dma_start(out=stage[:, 0:NCST_MAT * 128], in_=cst_d), writes=B_actT, dma=True, semkey="cst")
    op("sp", lambda e: e.dma_start(out=stage[:, 2048:2560], in_=w_pool_d.rearrange("g c d -> c g d")), writes=B_actT, dma=True, semkey="cst")
    op("sp", lambda e: e.dma_start(out=small[:], in_=small_d), writes=[B["small"]], dma=True, semkey="small")
    op("dve", lambda e: e.tensor_copy(out=cbf[:, 0:2, :].rearrange("p a b -> p (a b)"), in_=stage[:, 0:256]), reads=B_actT, writes=[B["cbf"]])
    op("dve", lambda e: e.tensor_copy(out=cbf[:, 2:14, :].rearrange("p a b -> p (a b)"), in_=stage[:, 3 * 128:15 * 128]), reads=B_actT, writes=[B["cbf"]])
    op("dve", lambda e: e.tensor_copy(out=trif[:], in_=stage[:, 128:256]), reads=B_actT, writes=[B["trif"]])
    op("dve", lambda e: e.tensor_copy(out=e127f[:], in_=stage[:, 256:384]), reads=B_actT, writes=[B["e127f"]])
    op("dve", lambda e: e.tensor_tensor(out=wpool_bf[:].rearrange("p g d -> p (g d)"), in0=stage[:, 2048:2560], in1=small[:, SM_PS:SM_PS + 512], op=ALU.mult),
       reads=B_actT + [B["small"]], writes=[B["wpool"]])
    op("dve", lambda e: e.memset(Va[:, :, :, 64:65], 1.0), writes=B_Va)
    op("dve", lambda e: e.memset(KTcur[64:67, :, :], 1.0), writes=B_KTcur)
    op("dve", lambda e: e.memset(u_sb[:, 7, :], 0.0), writes=[B_u[7]])
    def cast_w(src, dst, rows, key):
        step = 512 if rows % 512 == 0 else 256
        for r0 in range(0, rows, step):
            op("pool", lambda e, r0=r0: e.dma_start(out=dst[r0:r0 + step, :], in_=src[r0:r0 + step, :]), writes=[B[key]], dma=True, semkey=key)

    for i_, (c0_, n_) in enumerate([(1024, 512), (1536, 520), (0, 512), (512, 512)]):
        op("pool", lambda e, i_=i_, c0_=c0_, n_=n_: e.dma_start(out=ring[i_][:, 0:8 * n_].rearrange("p (k n) -> p k n", k=8),
                                                              in_=w_in_d[:, c0_:c0_ + n_].rearrange("(k p) n -> p k n", p=128)),
           writes=[B_ring[i_]], dma=True, semkey="ring%dsw" % i_)
    cast_w(w_in_d, win_b, D, "win_b")
    cast_w(w_out_d, wout_b, D, "wout_b")
    cast_w(w_g_d, wg_b, D, "wg_b")
    cast_w(w_u_d, wu_b, D, "wu_b")
    cast_w(w_d_d, wd_b, DFF, "wd_b")
    cast_w(w_ple_d, wple_b, DPLE, "wple_b")
    cast_w(w_gate_d, wgate_b, D, "wgate_b")

    ring_ctr = [0]
    grp_ctr = [0]

    def ring_load(dmas, srcbufs, slot=None):
        if slot is None:
            i = ring_ctr[0] % RING_N
            ring_ctr[0] += 1
        else:
            i = slot
        slot = ring[i]
        grp_ctr[0] += 1
        for f in dmas:
            o_ap, i_ap = f(slot)
            for j in range(o_ap.shape[1]):
                op("sp", lambda e, o_ap=o_ap, i_ap=i_ap, j=j: e.dma_start(out=o_ap[:, j, :], in_=i_ap[:, j, :]), reads=srcbufs, writes=[B_ring[i]], dma=True,
                   semkey="ring%d" % i, grp=("ring", grp_ctr[0]))
        return slot, B_ring[i]

    def w_kc(src, c0, n):
        return src[:, c0:c0 + n].rearrange("(k p) n -> p k n", p=128)

    st_ctr = [0]

    def gb_load(which):
        op("sp", lambda e: e.dma_start(out=gb[:], in_=gbc_d[:, which * D:(which + 1) * D]), writes=[B["gb"]], dma=True, semkey="gb")

    def rstd_of(src_ap, src_bufs, n, junk_ap, junk_bufs):
        i = st_ctr[0] % 8
        st_ctr[0] += 1
        sc = float(1.0 / np.sqrt(n))
        op("act", lambda e: e.activation(out=junk_ap, in_=src_ap, func=AF.Square, scale=sc, accum_out=st_ms[:, i:i + 1]),
           reads=src_bufs, writes=junk_bufs + [B_st[i]])
        op("act", lambda e: e.activation(out=st_ln[:, i:i + 1], in_=st_ms[:, i:i + 1], func=AF.Ln, bias=EPS), reads=[B_st[i]], writes=[B_st[i]])
        op("act", lambda e: e.activation(out=st_r[:, i:i + 1], in_=st_ln[:, i:i + 1], func=AF.Exp, scale=-0.5), reads=[B_st[i]], writes=[B_st[i]])
        return st_r[:, i:i + 1], B_st[i]

    def transpose_to_hT(src_bf, src_bufs, nk, dst3, dst_bufs, gcol=None, TBANK=7):
        psT = ps[:, TBANK, :].bitcast(BF16)
        for kc in range(nk):
            op("pe", lambda e, kc=kc: e.transpose(out=psT[:, kc * 128:(kc + 1) * 128], in_=src_bf[:, kc * 128:(kc + 1) * 128], identity=cbf[:, 0, :]),
               reads=src_bufs + [B["cbf"]], writes=[bk[TBANK]])
        src3 = psT[:, 0:nk * 128].rearrange("p (k t) -> p k t", k=nk)
        if gcol is None:
            op("dve", lambda e: e.tensor_copy(out=dst3, in_=src3), reads=[bk[TBANK]], writes=dst_bufs)
        else:
            g3 = small[:, gcol:gcol + nk].unsqueeze(2).broadcast_to([128, nk, 128])
            op("dve", lambda e: e.tensor_tensor(out=dst3, in0=src3, in1=g3, op=ALU.mult), reads=[bk[TBANK], B["small"]], writes=dst_bufs)

    rot = [0]

    def nbank(lo, hi):
        b = lo + rot[0] % (hi - lo)
        rot[0] += 1
        return b

    pre_done = set()

    def P1_pre(I):
        ring_load([lambda s: (s[:, 0:4096].rearrange("p (k n) -> p k n", k=8), w_kc(win_b, 1024, 512))], [B["win_b"]], slot=0)
        for c in range(2):
            T = I * 4 + c
            op("sp", lambda e, T=T, c=c: e.dma_start(out=xs[c][:], in_=x_d[T * 128:(T + 1) * 128, :]), writes=[B_xs[c]], dma=True, semkey="xs%d" % c)
        pre_done.add(I)

    def P1c(I, c, wv, wfu, bv, bfu):
        X, Y = 4 + 2 * c, 5 + 2 * c
        hb, fb, e1, nlf, r1, r2, csp = hbA[c], fbA[c], e1A[c], nlfA[c], r1A[c], r2A[c], cspA[c]
        Bhb, Bfb, Be1, Bnlf, Br1, Br2, Bcsp = B["hb%d" % c], B["fb%d" % c], B["e1%d" % c], B["nlf%d" % c], B["r1%d" % c], B["r2%d" % c], B["csp%d" % c]
        for s in (c, c + 2):
            T = I * 4 + s
            xi = c
            if not (I in pre_done and s == c):
                op("sp", lambda e, T=T, xi=xi: e.dma_start(out=xs[xi][:], in_=x_d[T * 128:(T + 1) * 128, :]), writes=[B_xs[xi]], dma=True, semkey="xs%d" % xi)
            r, rb = rstd_of(xs[xi][:], [B_xs[xi]], D, hb[:], [Bhb])
            op("dve", lambda e, xi=xi, r=r: e.tensor_scalar(out=hb[:], in0=xs[xi][:], scalar1=r, scalar2=None, op0=ALU.mult), reads=[B_xs[xi], rb], writes=[Bhb])
            yield 1
            transpose_to_hT(hb, [Bhb], 8, hT[:, :, s * 128:(s + 1) * 128], [B_hT[s]] + B_actT[0:8], gcol=SM_GPRE, TBANK=X)
            yield 1
            for kc in range(8):
                op("pe", lambda e, kc=kc, s=s: e.matmul(ps[:, X, :], lhsT=hT[:, kc, s * 128:(s + 1) * 128], rhs=wv[:, kc, :], start=(kc == 0), stop=(kc == 7)),
                   reads=[B_hT[s], bv], writes=[bk[X]])
            op("act", lambda e, T=T: e.copy(out=Va[:, T, :, 0:64], in_=ps[:, X, :].rearrange("p (h d) -> p h d", h=NH)), reads=[bk[X]], writes=[B_Va[T]])
            for kc in range(8):
                op("pe", lambda e, kc=kc, s=s: e.matmul(ps[:, Y, :], lhsT=hT[:, kc, s * 128:(s + 1) * 128], rhs=wfu[:, kc, 8:520], start=(kc == 0), stop=(kc == 7)),
                   reads=[B_hT[s], bfu], writes=[bk[Y]])
            op("dve", lambda e, T=T: e.tensor_copy(out=u_sb[:, T % 8, :], in_=ps[:, Y, :]), reads=[bk[Y]], writes=[B_u[T % 8]])
            yield 1
            for kc in range(8):
                op("pe", lambda e, kc=kc, s=s: e.matmul(ps[:, X, 0:8], lhsT=hT[:, kc, s * 128:(s + 1) * 128], rhs=wfu[:, kc, 0:8], start=(kc == 0), stop=(kc == 7)),
                   reads=[B_hT[s], bfu], writes=[bk[X]])
            op("dve", lambda e: e.tensor_tensor(out=fb[:], in0=ps[:, X, 0:8], in1=small[:, SM_B:SM_B + 8], op=ALU.add), reads=[bk[X], B["small"]], writes=[Bfb])
            op("act", lambda e: e.activation(out=e1[:], in_=fb[:], func=AF.Exp, scale=-1.0), reads=[Bfb], writes=[Be1])
            op("act", lambda e: e.activation(out=nlf[:], in_=e1[:], func=AF.Ln, bias=1.0), reads=[Be1], writes=[Bnlf])
            yield 1
            op("pe", lambda e, T=T: e.matmul(ps[:, X, 8:16], lhsT=trif[:], rhs=nlf[:], start=True, stop=(T == 0)), reads=[B["trif"], Bnlf], writes=[bk[X]])
            if T > 0:
                op("pe", lambda e, T=T: e.matmul(ps[:, X, 8:16], lhsT=e127f[:], rhs=negc[:, T - 1, :], start=False, stop=True),
                   reads=[B["e127f"], B_negc[T - 1]], writes=[bk[X]])
            op("dve", lambda e, T=T: e.tensor_copy(out=negc[:, T, :], in_=ps[:, X, 8:16]), reads=[bk[X]], writes=[B_negc[T]])
            yield 1
            op("dve", lambda e, T=T: e.tensor_scalar(out=csp[:, :, 0], in0=negc[:, T, :], scalar1=-1.0, scalar2=None, op0=ALU.mult), reads=[B_negc[T]], writes=[Bcsp])
            op("dve", lambda e, T=T: e.tensor_tensor(out=r1[:], in0=negc[:, T, :], in1=csp[:, :, 0], op=ALU.add), reads=[B_negc[T], Bcsp], writes=[Br1])
            op("dve", lambda e: e.tensor_scalar(out=csp[:, :, 1], in0=r1[:], scalar1=-1.0, scalar2=None, op0=ALU.mult), reads=[Br1], writes=[Bcsp])
            op("dve", lambda e: e.tensor_tensor(out=r2[:], in0=r1[:], in1=csp[:, :, 1], op=ALU.add), reads=[Br1, Bcsp], writes=[Br2])
            op("dve", lambda e: e.tensor_scalar(out=csp[:, :, 2], in0=r2[:], scalar1=-1.0, scalar2=None, op0=ALU.mult), reads=[Br2], writes=[Bcsp])
            yield 1
            for hg in range(2):
                cb = Y if hg == 0 else X
                for hh in range(4):
                    h = hg * 4 + hh
                    op("pe", lambda e, h=h, hh=hh, cb=cb: e.matmul(ps[64:67, cb, hh * 128:(hh + 1) * 128], lhsT=csp[:, h, :], rhs=cbf[:, 0, :], start=True, stop=True),
                       reads=[Bcsp, B["cbf"]], writes=[bk[cb]])
                op("act", lambda e, hg=hg, cb=cb, s=s: e.copy(out=QTa[64:67, hg * 4:hg * 4 + 4, s * 128:(s + 1) * 128],
                                                             in_=ps[64:67, cb, :].rearrange("p (h t) -> p h t", h=4)),
                   reads=[bk[cb]], writes=[B_QTa[hg * 4 + i] for i in range(4)])
            yield 1

    def P1(I):
        if I not in pre_done and I > 0:
            ring_load([lambda s: (s[:, 0:4096].rearrange("p (k n) -> p k n", k=8), w_kc(win_b, 1024, 512))], [B["win_b"]], slot=0)
        slot_v, bv = ring[0], B_ring[0]
        if I > 0:
            ring_load([lambda s: (s[:, 0:8 * 520].rearrange("p (k n) -> p k n", k=8), w_kc(win_b, 1536, 520))], [B["win_b"]], slot=1)
        slot_fu, bfu = ring[1], B_ring[1]
        wv = slot_v[:, 0:4096].rearrange("p (k n) -> p k n", k=8)
        wfu = slot_fu[:, 0:8 * 520].rearrange("p (k n) -> p k n", k=8)
        yield from merge_gen([P1c(I, 0, wv, wfu, bv, bfu), P1c(I, 1, wv, wfu, bv, bfu)], [14, 14])
        qs, ks = (0, 1) if I > 0 else (2, 3)
        if I > 0:
            ring_load([lambda s_: (s_[:, 0:4096].rearrange("p (k n) -> p k n", k=8), w_kc(win_b, 0, 512))], [B["win_b"]], slot=0)
            ring_load([lambda s_: (s_[:, 0:4096].rearrange("p (k n) -> p k n", k=8), w_kc(win_b, 512, 512))], [B["win_b"]], slot=1)
        wq = ring[qs][:, 0:4096].rearrange("p (k n) -> p k n", k=8)
        wk = ring[ks][:, 0:4096].rearrange("p (k n) -> p k n", k=8)
        bq, bkk = B_ring[qs], B_ring[ks]
        for j in range(4):
            qb = 4 + (2 * j) % 4
            for kc in range(8):
                op("pe", lambda e, kc=kc, j=j, qb=qb: e.matmul(ps[:, qb, :], lhsT=wq[:, kc, j * 128:(j + 1) * 128], rhs=hT[:, kc, :], start=(kc == 0), stop=(kc == 7)),
                   reads=B_hT + [bq], writes=[bk[qb]])
            op("act", lambda e, j=j, qb=qb: e.activation(out=QTa[0:64, 2 * j, :], in_=ps[0:64, qb, :], func=AF.Copy, scale=0.125), reads=[bk[qb]], writes=[B_QTa[2 * j]])
            op("act", lambda e, j=j, qb=qb: e.activation(out=QTa[0:64, 2 * j + 1, :], in_=ps[64:128, qb, :], func=AF.Copy, scale=0.125), reads=[bk[qb]], writes=[B_QTa[2 * j + 1]])
            yield 1
            kb = 4 + (2 * j + 1) % 4
            for kc in range(8):
                op("pe", lambda e, kc=kc, j=j, kb=kb: e.matmul(ps[:, kb, :], lhsT=wk[:, kc, j * 128:(j + 1) * 128], rhs=hT[:, kc, :], start=(kc == 0), stop=(kc == 7)),
                   reads=B_hT + [bkk], writes=[bk[kb]])
            op("dve", lambda e, j=j, kb=kb: e.tensor_copy(out=KTcur[0:64, 2 * j, :], in_=ps[0:64, kb, :]), reads=[bk[kb]], writes=[B_KTcur[2 * j]])
            op("dve", lambda e, j=j, kb=kb: e.tensor_copy(out=KTcur[0:64, 2 * j + 1, :], in_=ps[64:128, kb, :]), reads=[bk[kb]], writes=[B_KTcur[2 * j + 1]])
            yield 1
        if I < NBLK - 1:
            grp_ctr[0] += 1
            for h_ in range(NH):
                op("pool", lambda e, I=I, h_=h_: e.dma_start(out=kt_d[h_, :, I * TB:(I + 1) * TB], in_=KTcur[0:67, h_, :]),
                   reads=B_KTcur, writes=[B_ktd[I]], dma=True, semkey="ktw", grp=("ktw", grp_ctr[0]))

    att_ctr = [0]
    kts_done = set()

    def kts_issue(I, h):
        if I == 0 or (I, h) in kts_done:
            return
        kts_done.add((I, h))
        kb_ = h % 2
        grp_ctr[0] += 1
        for c0 in range(0, I * TB, 1024):
            c1 = min(c0 + 1024, I * TB)
            op("pool", lambda e, c0=c0, c1=c1: e.dma_start(out=KTs[kb_][0:67, c0:c1], in_=kt_d[h, :, c0:c1]),
               reads=B_ktd[0:I], writes=[B_KTs[kb_]], dma=True, semkey="kts%d" % kb_, grp=("kts", grp_ctr[0]))


    def att_gen(I):
        nprev = 4 * I
        items = [(h, j) for h in range(NH) for j in range(nprev + 4)]

        def emit_S(h, j, idx):
            jj = j - nprev
            q_lo = max(0, jj) * 128
            sbk = idx % 3
            pti = idx % 3
            kb_ = h % 2
            if j < nprev:
                lhs = KTs[kb_][0:67, j * 128:(j + 1) * 128]
                lb = [B_KTs[kb_]]
            else:
                lhs = KTcur[0:67, h, jj * 128:(jj + 1) * 128]
                lb = [B_KTcur[h]]
            op("pe", lambda e: e.matmul(ps[:, sbk, q_lo:TB], lhsT=lhs, rhs=QTa[0:67, h, q_lo:TB], start=True, stop=True),
               reads=lb + [B_QTa[h]], writes=[bk[sbk]])
            op("act", lambda e: e.activation(out=PT[pti][:, q_lo:TB], in_=ps[:, sbk, q_lo:TB], func=AF.Exp, bias=negc[:, j, h:h + 1], scale=1.0),
               reads=[bk[sbk], B_negc[j]], writes=[B_PT[pti]])
            if jj >= 0:
                op("dve", lambda e: e.tensor_tensor(out=PT[pti][:, q_lo:q_lo + 128], in0=PT[pti][:, q_lo:q_lo + 128], in1=cbf[:, 1, :], op=ALU.mult),
                   reads=[B_PT[pti], B["cbf"]], writes=[B_PT[pti]])

        def emit_PV(h, j, idx):
            jj = j - nprev
            pti = idx % 3
            ob = 3
            Ov = ps[:, ob, 0:260].rearrange("p (s c) -> p s c", s=4)
            for ii in range(max(0, jj), 4):
                op("pe", lambda e, ii=ii: e.matmul(Ov[:, ii, :], lhsT=PT[pti][:, ii * 128:(ii + 1) * 128], rhs=Va[:, j, h, :],
                                                  start=(j == 0 and ii == 0), stop=(j == nprev + 3 and ii == 3)),
                   reads=[B_PT[pti], B_Va[j]], writes=[bk[ob]])
            if j == nprev + 3:
                op("dve", lambda e: e.reciprocal(out=rl[:], in_=Ov[:, :, 64]), reads=[bk[ob]], writes=[B["rl"]])
                op("dve", lambda e: e.tensor_tensor(out=A_sb[:, :, h * 64:(h + 1) * 64], in0=Ov[:, :, 0:64],
                                                    in1=rl[:, :].unsqueeze(2).broadcast_to([128, 4, 64]), op=ALU.mult),
                   reads=[bk[ob], B["rl"]], writes=B_A)

        def kts_load(h):
            kts_issue(I, h)

        base = att_ctr[0]
        att_ctr[0] += len(items)
        kts_load(0)
        for n, (h, j) in enumerate(items):
            if j == 0 and h + 1 < NH:
                kts_load(h + 1)
            if n == 0:
                emit_S(h, j, base + n)
                if len(items) > 1:
                    emit_S(items[1][0], items[1][1], base + 1)
            if n + 2 < len(items):
                h2, j2 = items[n + 2]
                emit_S(h2, j2, base + n + 2)
            emit_PV(h, j, base + n)
            yield 0.55

    def P3ac(I, c, wo, bo):
        X, Y = 2 * c, 2 * c + 1
        hb2, t32c, yTs = hb2A[c], t32A[c], yTsA[c]
        Bhb2, Bt32, ByTs = B["hb2_%d" % c], B["t32_%d" % c], B["yTs%d" % c]
        for s in (c, c + 2):
            T = I * 4 + s
            for g in range(4):
                bt = (C_BT0 if T == 0 else C_BT) + g - 1
                bp = C_BP + g - 1
                op("pe", lambda e, g=g, T=T, bt=bt: e.matmul(ps[:, X, g * 128:(g + 1) * 128], lhsT=u_sb[:, T % 8, g * 128:(g + 1) * 128], rhs=cbf[:, bt, :], start=True, stop=False),
                   reads=[B_u[T % 8], B["cbf"]], writes=[bk[X]])
                op("pe", lambda e, g=g, T=T, bp=bp: e.matmul(ps[:, X, g * 128:(g + 1) * 128], lhsT=u_sb[:, (T - 1) % 8, g * 128:(g + 1) * 128], rhs=cbf[:, bp, :], start=False, stop=True),
                   reads=[B_u[(T - 1) % 8], B["cbf"]], writes=[bk[X]])
            op("act", lambda e: e.copy(out=yTs[:], in_=ps[:, X, :]), reads=[bk[X]], writes=[ByTs])
            yield 1
            for g in range(4):
                op("pe", lambda e, g=g: e.matmul(ps[:, Y, g * 128:(g + 1) * 128], lhsT=yTs[:, g * 128:(g + 1) * 128], rhs=wpool_bf[:, g, :], start=True, stop=True),
                   reads=[ByTs, B["wpool"]], writes=[bk[Y]])
            rA, rAb = rstd_of(A_sb[:, s, :], [B_A[s]], DATT, hb2[:, 0:512], [Bhb2])
            op("act", lambda e, s=s, rA=rA: e.activation(out=hb2[:, 0:512], in_=A_sb[:, s, :], func=AF.Copy, scale=rA), reads=[B_A[s], rAb], writes=[Bhb2])
            yield 1
            rB, rBb = rstd_of(ps[:, Y, :], [bk[Y]], DPOOL, hb2[:, 512:1024], [Bhb2])
            op("act", lambda e, rB=rB: e.activation(out=hb2[:, 512:1024], in_=ps[:, Y, :], func=AF.Copy, scale=rB), reads=[bk[Y], rBb], writes=[Bhb2])
            yield 1
            transpose_to_hT(hb2, [Bhb2], 8, hT2[:, :, s * 128:(s + 1) * 128], [B_hT2[s]], gcol=SM_GMIX, TBANK=X)
            yield 1
            for half in range(2):
                for kc in range(8):
                    op("pe", lambda e, kc=kc, s=s, half=half: e.matmul(ps[:, X + half, :], lhsT=hT2[:, kc, s * 128:(s + 1) * 128], rhs=wo[half][:, kc, :],
                                                                    start=(kc == 0), stop=(kc == 7)),
                       reads=[B_hT2[s], bo[half]], writes=[bk[X + half]])
            pair = ps[:, X:X + 2, :].rearrange("p a n -> p (a n)")
            yield 1
            r, rb = rstd_of(pair, [bk[X], bk[Y]], D, t32c[:], [Bt32])
            yield 1
            op("dve", lambda e, pair=pair, r=r: e.scalar_tensor_tensor(out=t32c[:], in0=pair, scalar=r, in1=gb[:], op0=ALU.mult, op1=ALU.mult),
               reads=[bk[X], bk[Y], rb, B["gb"]], writes=[Bt32])
            op("dve", lambda e, s=s: e.tensor_tensor(out=xres[:, s, :], in0=xres[:, s, :], in1=t32c[:], op=ALU.add), reads=[B_x[s], Bt32], writes=[B_x[s]])
            yield 1
            r, rb = rstd_of(xres[:, s, :], [B_x[s]], D, hb2[:], [Bhb2])
            op("dve", lambda e, s=s, r=r: e.tensor_scalar(out=hb2[:], in0=xres[:, s, :], scalar1=r, scalar2=None, op0=ALU.mult), reads=[B_x[s], rb], writes=[Bhb2])
            yield 1
            transpose_to_hT(hb2, [Bhb2], 8, hT2[:, :, s * 128:(s + 1) * 128], [B_hT2[s]], gcol=SM_GFFN, TBANK=X)
            yield 1

    def P3a(I):
        if I == 0:
            load_xp(I)
        slot_o0, bo0 = ring_load([lambda s: (s[:, 0:4096].rearrange("p (k n) -> p k n", k=8), w_kc(wout_b, 0, 512))], [B["wout_b"]], slot=2)
        slot_o1, bo1 = ring_load([lambda s: (s[:, 0:4096].rearrange("p (k n) -> p k n", k=8), w_kc(wout_b, 512, 512))], [B["wout_b"]], slot=3)
        wo = [slot_o0[:, 0:4096].rearrange("p (k n) -> p k n", k=8), slot_o1[:, 0:4096].rearrange("p (k n) -> p k n", k=8)]
        bo = [bo0, bo1]
        yield from merge_gen([P3ac(I, 0, wo, bo), P3ac(I, 1, wo, bo)], [18, 18])

    def P3bcd_gen(I):
        for pc in range(NFC // 2):
            c0 = pc * 256
            if pc == 2:
                gb_load(1)
            slot_gu, bgu = ring_load([lambda s, c0=c0: (s[:, 0:2048].rearrange("p (k n) -> p k n", k=8), w_kc(wg_b, c0, 256)),
                                      lambda s, c0=c0: (s[:, 2048:4096].rearrange("p (k n) -> p k n", k=8), w_kc(wu_b, c0, 256))], [B["wg_b"], B["wu_b"]])
            wgv = slot_gu[:, 0:2048].rearrange("p (k n) -> p k n", k=8)
            wuv = slot_gu[:, 2048:4096].rearrange("p (k n) -> p k n", k=8)
            for m in range(2):
                c = pc * 2 + m
                gbk = 4 + 2 * (c % 2)
                for kc in range(8):
                    op("pe", lambda e, kc=kc, m=m, gbk=gbk, wgv=wgv: e.matmul(ps[:, gbk, :], lhsT=wgv[:, kc, m * 128:(m + 1) * 128], rhs=hT2[:, kc, :], start=(kc == 0), stop=(kc == 7)),
                       reads=B_hT2 + [bgu], writes=[bk[gbk]])
                for kc in range(8):
                    op("pe", lambda e, kc=kc, m=m, gbk=gbk, wuv=wuv: e.matmul(ps[:, gbk + 1, :], lhsT=wuv[:, kc, m * 128:(m + 1) * 128], rhs=hT2[:, kc, :], start=(kc == 0), stop=(kc == 7)),
                       reads=B_hT2 + [bgu], writes=[bk[gbk + 1]])
                si = c % 2
                op("act", lambda e, gbk=gbk, si=si: e.activation(out=sg32[si], in_=ps[:, gbk, :], func=AF.Exp, scale=-1.0), reads=[bk[gbk]], writes=[B["sg%d" % si]])
                op("act", lambda e, si=si: e.activation(out=sg32[si], in_=sg32[si], func=AF.Ln, bias=1.0), reads=[B["sg%d" % si]], writes=[B["sg%d" % si]])
                op("act", lambda e, si=si: e.activation(out=sg32[si], in_=sg32[si], func=AF.Exp, scale=-1.0), reads=[B["sg%d" % si]], writes=[B["sg%d" % si]])
                op("dve", lambda e, gbk=gbk, si=si: e.tensor_tensor(out=sg32[si], in0=sg32[si], in1=ps[:, gbk, :], op=ALU.mult),
                   reads=[B["sg%d" % si], bk[gbk]], writes=[B["sg%d" % si]])
                op("dve", lambda e, gbk=gbk, si=si, c=c: e.tensor_tensor(out=actT[:, c, :], in0=sg32[si], in1=ps[:, gbk + 1, :], op=ALU.mult),
                   reads=[B["sg%d" % si], bk[gbk + 1]], writes=[B_actT[c]] + (B_hT if c < 8 else []))
                yield 3.5
        def p_load(s_):
            T_ = I * 4 + s_
            op("sp", lambda e: e.dma_start(out=ptile[:, T_ % 2, :], in_=p_d[T_ * 128:(T_ + 1) * 128, :]), writes=[B_pt[T_ % 2]], dma=True, semkey="p%d" % (T_ % 2))

        for pr in range(2):
            if pr == 1:
                p_load(0)
                p_load(1)
            for pc in range(6):
                nch = 4 if pc < 5 else 2
                slot_d, bd = ring_load([lambda s, pc=pc, nch=nch: (s[:, 0:nch * 1024].rearrange("p (c n) -> p c n", c=nch),
                                                                  wd_b[pc * 512:pc * 512 + nch * 128, :].rearrange("(c p) n -> p c n", p=128))], [B["wd_b"]])
                wdv = slot_d[:, 0:nch * 1024].rearrange("p (c n) -> p c n", c=nch)
                for cc in range(nch):
                    c = pc * 4 + cc
                    for si in range(2):
                        s = pr * 2 + si
                        for half in range(2):
                            op("pe", lambda e, c=c, cc=cc, s=s, si=si, half=half, wdv=wdv: e.matmul(ps[:, 4 + si * 2 + half, :], lhsT=actT[:, c, s * 128:(s + 1) * 128],
                                                                                                 rhs=wdv[:, cc, half * 512:(half + 1) * 512], start=(c == 0), stop=(c == NFC - 1)),
                               reads=[B_actT[c], bd], writes=[bk[4 + si * 2 + half]])
                    yield 1.6
            if pr == 1:
                slot_p, bp_ = ring_load([lambda s: (s[:, 0:2048].rearrange("p (k n) -> p k n", k=2), wple_b.rearrange("(k p) n -> p k n", p=128))], [B["wple_b"]], slot=1)
                slot_g0, bg0 = ring_load([lambda s: (s[:, 0:4096].rearrange("p (k n) -> p k n", k=8), w_kc(wgate_b, 0, 512))], [B["wgate_b"]], slot=2)
                slot_g1, bg1 = ring_load([lambda s: (s[:, 0:4096].rearrange("p (k n) -> p k n", k=8), w_kc(wgate_b, 512, 512))], [B["wgate_b"]], slot=3)
            for si in range(2):
                s = pr * 2 + si
                b0 = 4 + si * 2
                pair = ps[:, b0:b0 + 2, :].rearrange("p a n -> p (a n)")
                r, rb = rstd_of(pair, [bk[b0], bk[b0 + 1]], D, t32[:], [B["t32_0"]])
                op("dve", lambda e, pair=pair, r=r: e.scalar_tensor_tensor(out=t32[:], in0=pair, scalar=r, in1=gb[:], op0=ALU.mult, op1=ALU.mult),
                   reads=[bk[b0], bk[b0 + 1], rb, B["gb"]], writes=[B["t32_0"]])
                op("dve", lambda e, s=s: e.tensor_tensor(out=xres[:, s, :], in0=xres[:, s, :], in1=t32[:], op=ALU.add), reads=[B_x[s], B["t32_0"]], writes=[B_x[s]])
                yield 4.0
            if pr == 1:
                gb_load(2)

        wpl = slot_p[:, 0:2048].rearrange("p (k n) -> p k n", k=2)
        wgt = [slot_g0[:, 0:4096].rearrange("p (k n) -> p k n", k=8), slot_g1[:, 0:4096].rearrange("p (k n) -> p k n", k=8)]
        bgt = [bg0, bg1]
        for s in range(4):
            T = I * 4 + s
            pi = T % 2
            op("act", lambda e, pi=pi: e.copy(out=pb[:], in_=ptile[:, pi, :]), reads=[B_pt[pi]], writes=[B["pb"]])
            if s < 2:
                p_load(s + 2)
            transpose_to_hT(pb, [B["pb"]], 2, pT[:, :, :], [B["pT"]])
            for half in range(2):
                for kc in range(2):
                    op("pe", lambda e, kc=kc, half=half: e.matmul(ps[:, 4 + half, :], lhsT=pT[:, kc, :], rhs=wpl[:, kc, half * 512:(half + 1) * 512], start=(kc == 0), stop=(kc == 1)),
                       reads=[B["pT"], bp_], writes=[bk[4 + half]])
            pairE = ps[:, 4:6, :].rearrange("p a n -> p (a n)")
            rE, rEb = rstd_of(pairE, [bk[4], bk[5]], D, t32b[:], [B["t32b"], B["sg0"], B["sg1"]])
            op("dve", lambda e, pairE=pairE, rE=rE: e.scalar_tensor_tensor(out=t32b[:], in0=pairE, scalar=rE, in1=gb[:], op0=ALU.mult, op1=ALU.mult),
               reads=[bk[4], bk[5], rEb, B["gb"]], writes=[B["t32b"], B["sg0"], B["sg1"]])
            if s == 0 and I + 2 < NBLK:
                P1_pre(I + 2)
            yield 5.0
            op("dve", lambda e, s=s: e.tensor_copy(out=hb2A[0][:], in_=xres[:, s, :]), reads=[B_x[s]], writes=[B["hb2_0"]])
            transpose_to_hT(hb2A[0], [B["hb2_0"]], 8, hT2[:, :, s * 128:(s + 1) * 128], [B_hT2[s]])
            for half in range(2):
                for kc in range(8):
                    op("pe", lambda e, kc=kc, s=s, half=half: e.matmul(ps[:, 6 + half, :], lhsT=hT2[:, kc, s * 128:(s + 1) * 128], rhs=wgt[half][:, kc, :],
                                                                    start=(kc == 0), stop=(kc == 7)),
                       reads=[B_hT2[s], bgt[half]], writes=[bk[6 + half]])
            pairG = ps[:, 6:8, :].rearrange("p a n -> p (a n)")
            op("act", lambda e, pairG=pairG: e.activation(out=t32[:], in_=pairG, func=AF.Exp, scale=-1.0), reads=[bk[6], bk[7]], writes=[B["t32_0"]])
            op("act", lambda e: e.activation(out=t32[:], in_=t32[:], func=AF.Ln, bias=1.0), reads=[B["t32_0"]], writes=[B["t32_0"]])
            op("act", lambda e: e.activation(out=t32[:], in_=t32[:], func=AF.Exp, scale=-1.0), reads=[B["t32_0"]], writes=[B["t32_0"]])
            op("dve", lambda e: e.tensor_tensor(out=t32[:], in0=t32[:], in1=t32b[:], op=ALU.mult), reads=[B["t32_0"], B["t32b"], B["sg0"], B["sg1"]], writes=[B["t32_0"]])
            op("dve", lambda e, s=s: e.tensor_tensor(out=xres[:, s, :], in0=xres[:, s, :], in1=t32[:], op=ALU.add), reads=[B_x[s], B["t32_0"]], writes=[B_x[s]])
            op("pool", lambda e, s=s, T=T: e.dma_start(out=y_d[T * 128:(T + 1) * 128, :], in_=xres[:, s, :]), reads=[B_x[s]], writes=[B_y[s]], dma=True, semkey="y%d" % s)
            if I + 1 < NBLK:
                op("pool", lambda e, s=s, T=T: e.dma_start(out=xres[:, s, :], in_=x_d[(T + 4) * 128:(T + 5) * 128, :]), writes=[B_x[s]], dma=True, semkey="x%d" % s)
            if s == 3 and I + 1 < NBLK:
                gb_load(0)
            yield 14.0

    def merge_gen(gens, tots):
        t = [0.0] * len(gens)
        live = [g is not None for g in gens]
        while any(live):
            k = min((i for i in range(len(gens)) if live[i]), key=lambda i: (t[i] / tots[i], i))
            try:
                c = next(gens[k])
                t[k] += c
                yield c
            except StopIteration:
                live[k] = False

    def merge(ga, ta_tot, gb, tb_tot):
        ta = tb = 0.0
        a_done = ga is None
        b_done = gb is None
        while not (a_done and b_done):
            if (not a_done) and (b_done or ta / ta_tot <= tb / tb_tot):
                try:
                    ta += next(ga)
                except StopIteration:
                    a_done = True
            else:
                try:
                    tb += next(gb)
                except StopIteration:
                    b_done = True

    TB_COST = 11 * 2 * 3.5 + 2 * (22 * 1.6 + 8.0) + 4 * 19.0
    NY1 = 28 + 8
    NY3 = 26
    gb_load(0)
    for _ in P1(0):
        pass
    merge(att_gen(0), 1.0, None, 1.0)
    for I in range(NBLK):
        if I + 1 < NBLK:
            kts_issue(I + 1, 0)
            kts_issue(I + 1, 1)
            merge(P1(I + 1), NY1, P3a(I), NY3)
            merge(att_gen(I + 1), NH * (4 * (I + 1) + 4) * 0.55, P3bcd_gen(I), TB_COST)
        else:
            merge(P3a(I), 1.0, None, 1.0)
            merge(None, 1.0, P3bcd_gen(I), TB_COST)

    S.emit()
    return nc


def make_maps(inputs, S_LEN, n_cores):
    f = lambda a: np.ascontiguousarray(np.asarray(a, dtype=np.float32))
    cst = host_consts()
    small = np.zeros((128, NSMALL), np.float32)
    small[:, SM_B:SM_B + 8] = np.broadcast_to(f(inputs["b_forget"])[0][None, :], (128, 8))
    small[:, SM_PS:SM_PS + 512] = np.broadcast_to(f(inputs["pool_scale"])[0][None, :], (128, 512))
    small[:, SM_GPRE:SM_GPRE + 8] = f(inputs["g_mix_pre"])[0].reshape(8, 128).T
    gmix = np.concatenate([f(inputs["g_attn_grp"])[0], f(inputs["g_pool_grp"])[0]])
    small[:, SM_GMIX:SM_GMIX + 8] = gmix.reshape(8, 128).T
    small[:, SM_GFFN:SM_GFFN + 8] = f(inputs["g_ffn_pre"])[0].reshape(8, 128).T
    gbc = np.concatenate([f(inputs["g_mix_post"])[0], f(inputs["g_ffn_post"])[0], f(inputs["g_ple"])[0]])
    gbc = np.ascontiguousarray(np.broadcast_to(gbc[None, :], (128, 3 * D)))
    shared = {
        "w_in": f(inputs["w_in"])[0], "w_pool": f(inputs["w_pool"])[0], "w_out": f(inputs["w_out"])[0],
        "w_ffn_gate": f(inputs["w_ffn_gate"])[0], "w_ffn_up": f(inputs["w_ffn_up"])[0], "w_ffn_down": f(inputs["w_ffn_down"])[0],
        "w_ple_proj": f(inputs["w_ple_proj"])[0], "w_ple_gate": f(inputs["w_ple_gate"])[0],
        "cst": cst, "small": small, "gbc": gbc,
    }
    x = f(inputs["x"])
    p = f(inputs["p"])[0]
    maps = []
    for c in range(n_cores):
        m = dict(shared)
        m["x"] = np.ascontiguousarray(x[c, :S_LEN])
        m["p"] = np.ascontiguousarray(p[c, :S_LEN])
        maps.append(m)
    return maps


_NC_CACHE = {}


def kernel(**inputs):
    S_LEN = 4096
    n = 8
    if S_LEN not in _NC_CACHE:
        _NC_CACHE[S_LEN] = build_nc(S_LEN)
    nc = _NC_CACHE[S_LEN]
    maps = make_maps(inputs, S_LEN, n)
    res = run_bass_kernel_spmd(nc, maps, core_ids=list(range(n)))
    return np.stack([np.asarray(r["y"], dtype=np.float32) for r in res.results], axis=0)
```

```python
import numpy as np
import concourse.bass as bass
import concourse.mybir as mybir
from concourse.bass_utils import run_bass_kernel_spmd

F32 = mybir.dt.float32
BF16 = mybir.dt.bfloat16
AF = mybir.ActivationFunctionType
ALU = mybir.AluOpType

D = 1024
DPLE = 256
NH = 8
HD = 64
DATT = 512
DPOOL = 512
DIN = 2056
DFF = 2816
NFC = DFF // 128
EPS = 1e-6
TB = 512
WINS = (2, 4, 8, 16)


class Buf:
    __slots__ = ("name", "w", "rs")

    def __init__(self, name):
        self.name = name
        self.w = None
        self.rs = []


class Op:
    __slots__ = ("eng", "fn", "deps", "needs_inc", "is_dma", "semkey", "sem", "semval", "grp")


class Sched:
    ENGS = ["pe", "act", "dve", "pool", "sp"]

    def __init__(self, nc):
        self.nc = nc
        self.ops = {e: [] for e in self.ENGS}
        self.dma_cum = {}

    def op(self, eng, fn, reads=(), writes=(), dma=False, semkey=None, grp=None):
        o = Op()
        o.eng = eng
        o.fn = fn
        o.is_dma = dma
        o.needs_inc = dma
        o.semkey = semkey
        o.grp = grp
        o.sem = None
        o.semval = None
        deps = []

        def add(d, raw):
            if d is o:
                return
            if grp is not None and d.is_dma and d.grp == grp:
                return
            if (not d.is_dma) and (not dma) and d.eng == eng and eng == "pe":
                return
            if d not in deps:
                deps.append(d)

        for b in reads:
            if b.w is not None:
                add(b.w, True)
        for b in writes:
            if b.w is not None:
                add(b.w, False)
            for r in b.rs:
                add(r, False)
        for b in reads:
            b.rs.append(o)
        for b in writes:
            b.w = o
            b.rs = []
        o.deps = deps
        for d in deps:
            d.needs_inc = True
        if dma:
            self.dma_cum[semkey] = self.dma_cum.get(semkey, 0) + 16
            o.semval = self.dma_cum[semkey]
        self.ops[eng].append(o)
        return o

    def emit(self):
        nc = self.nc
        esem = {}
        for e in ["pe", "act", "dve", "pool"]:
            esem[e] = nc.alloc_semaphore("prog_" + e)
            cnt = 0
            for o in self.ops[e]:
                if o.is_dma:
                    continue
                if o.needs_inc:
                    cnt += 1
                    o.sem = esem[e]
                    o.semval = cnt
        dsem = {k: nc.alloc_semaphore("dma_%s" % (k,)) for k in self.dma_cum}
        for e in self.ENGS:
            for o in self.ops[e]:
                if o.is_dma:
                    o.sem = dsem[o.semkey]
        final_waits = [(dsem[k], v) for k, v in self.dma_cum.items()]

        def run(ename, eng):
            waited = {}
            for o in self.ops[ename]:
                for d in o.deps:
                    key = d.sem.num
                    if waited.get(key, 0) >= d.semval:
                        continue
                    eng.wait_ge(d.sem, d.semval)
                    waited[key] = d.semval
                ins = o.fn(eng)
                if o.needs_inc:
                    ins.then_inc(o.sem, 16 if o.is_dma else 1)
            if ename == "sp":
                for s, v in final_waits:
                    if waited.get(s.num, 0) < v:
                        eng.wait_ge(s, v)

        with nc.Block() as block:
            @block.tensor
            def _(eng):
                run("pe", eng)

            @block.scalar
            def _(eng):
                run("act", eng)

            @block.vector
            def _(eng):
                run("dve", eng)

            @block.gpsimd
            def _(eng):
                run("pool", eng)

            @block.sync
            def _(eng):
                run("sp", eng)


NCST_MAT = 15
C_IDENT, C_TRI, C_E127, C_BT, C_BP, C_BT0 = 0, 1, 2, 3, 7, 11
SM_B, SM_PS, SM_GPRE, SM_GMIX, SM_GFFN = 0, 8, 520, 528, 536
NSMALL = 544


def host_consts():
    m = np.zeros((NCST_MAT, 128, 128), np.float32)
    i = np.arange(128)
    m[C_IDENT] = np.eye(128, dtype=np.float32)
    m[C_TRI] = (i[:, None] <= i[None, :]).astype(np.float32)
    m[C_E127][127, :] = 1.0
    for g, w in enumerate(WINS):
        s = i[:, None]
        t = i[None, :]
        band = ((t - s >= 0) & (t - s < w)).astype(np.float32)
        m[C_BT + g] = band / w - np.eye(128, dtype=np.float32)
        m[C_BP + g] = ((t + 128 - s) < w).astype(np.float32) / w
        cnt = np.minimum(i + 1, w).astype(np.float32)[None, :]
        m[C_BT0 + g] = band / cnt - np.eye(128, dtype=np.float32)
    return np.ascontiguousarray(m.transpose(1, 0, 2).reshape(128, NCST_MAT * 128))


def build_nc(S_LEN):
    NBLK = S_LEN // TB
    NT = S_LEN // 128
    nc = bass.Bass("TRN2", target_bir_lowering=False)
    dt_in = lambda n, s: nc.dram_tensor(n, s, F32, kind="ExternalInput").ap()
    x_d = dt_in("x", [S_LEN, D])
    p_d = dt_in("p", [S_LEN, DPLE])
    w_in_d = dt_in("w_in", [D, DIN])
    w_pool_d = dt_in("w_pool", [4, 128, 128])
    w_out_d = dt_in("w_out", [D, D])
    w_g_d = dt_in("w_ffn_gate", [D, DFF])
    w_u_d = dt_in("w_ffn_up", [D, DFF])
    w_d_d = dt_in("w_ffn_down", [DFF, D])
    w_ple_d = dt_in("w_ple_proj", [DPLE, D])
    w_gate_d = dt_in("w_ple_gate", [D, D])
    cst_d = dt_in("cst", [128, NCST_MAT * 128])
    small_d = dt_in("small", [128, NSMALL])
    gbc_d = dt_in("gbc", [128, 3 * D])
    y_d = nc.dram_tensor("y", [S_LEN, D], F32, kind="ExternalOutput").ap()

    scr = lambda n, s: nc.dram_tensor(n, s, BF16, kind="Internal").ap()
    win_b = scr("win_b", [D, DIN])
    wout_b = scr("wout_b", [D, D])
    wg_b = scr("wg_b", [D, DFF])
    wu_b = scr("wu_b", [D, DFF])
    wd_b = scr("wd_b", [DFF, D])
    wple_b = scr("wple_b", [DPLE, D])
    wgate_b = scr("wgate_b", [D, D])
    kt_d = scr("kt_d", [NH, 67, S_LEN])

    S = Sched(nc)
    sb = lambda n, s, d: nc.alloc_sbuf_tensor("s_" + n, s, d)

    Va = sb("Va", [128, NT, NH, 65], BF16)
    KTcur = sb("KTcur", [128, NH, TB], BF16)
    KTs = [sb("KTs%d" % i, [128, max(S_LEN - TB, 128)], BF16) for i in range(2)]
    QTa = sb("QTa", [128, NH, TB], BF16)
    xres = sb("xres", [128, 4, D], F32)
    actT = sb("actT", [128, NFC, TB], BF16)
    hT = actT[:, 0:8, :]
    A_sb = sb("A_sb", [128, 4, DATT], F32)
    gb = sb("gb", [128, D], F32)
    PT = [sb("PT%d" % i, [128, TB], BF16) for i in range(3)]
    pT = sb("pT", [128, 2, 128], BF16)
    u_sb = sb("u_sb", [128, 8, DPOOL], BF16)
    hbA = [sb("hb%d" % i, [128, D], BF16) for i in range(2)]
    hb2A = [sb("hb2_%d" % i, [128, D], BF16) for i in range(2)]
    hT2 = sb("hT2", [128, 8, TB], BF16)
    xs = [sb("xs%d" % i, [128, D], F32) for i in range(2)]
    t32A = [sb("t32_%d" % i, [128, D], F32) for i in range(2)]
    t32 = t32A[0]
    t32b = sb("t32b", [128, D], F32)
    sg32 = [t32b[:, 0:TB], t32b[:, TB:2 * TB]]
    yTsA = [sb("yTs%d" % i, [128, 512], BF16) for i in range(2)]
    ptile = sb("ptile", [128, 2, DPLE], F32)
    pb = sb("pb", [128, DPLE], BF16)
    RING_N = 4
    RING_E = 4224
    ring = [sb("ring%d" % i, [128, RING_E], BF16) for i in range(RING_N)]
    cbf = sb("cbf", [128, 14, 128], BF16)
    trif = sb("trif", [128, 128], F32)
    negm = sb("negm", [128, 128], BF16)
    e127f = sb("e127f", [128, 128], F32)
    small = sb("small", [128, NSMALL], F32)
    wpool_bf = sb("wpool_bf", [128, 4, 128], BF16)
    negc = sb("negc", [128, NT, NH], F32)
    fbA = [sb("fb%d" % i, [128, NH], F32) for i in range(2)]
    e1A = [sb("e1%d" % i, [128, NH], F32) for i in range(2)]
    nlfA = [sb("nlf%d" % i, [128, NH], F32) for i in range(2)]
    r1A = [sb("r1%d" % i, [128, NH], F32) for i in range(2)]
    r2A = [sb("r2%d" % i, [128, NH], F32) for i in range(2)]
    cspA = [sb("csp%d" % i, [128, NH, 3], BF16) for i in range(2)]
    st_ms = sb("st_ms", [128, 8], F32)
    st_ln = sb("st_ln", [128, 8], F32)
    st_r = sb("st_r", [128, 8], F32)
    rl = sb("rl", [128, 4], F32)

    ps = nc.alloc_psum_tensor("ps", [128, 8, 512], F32)

    bk = [Buf("bank%d" % i) for i in range(8)]
    B_Va = [Buf("Va%d" % i) for i in range(NT)]
    B_KTcur = [Buf("KTcur%d" % h) for h in range(NH)]
    B_KTs = [Buf("KTs0"), Buf("KTs1")]
    B_ktd = [Buf("ktd%d" % i) for i in range(NBLK)]
    B_QTa = [Buf("QTa%d" % h) for h in range(NH)]
    B_x = [Buf("x%d" % s) for s in range(4)]
    B_hT = [Buf("hT%d" % s) for s in range(4)]
    B_hT2 = [Buf("hT2_%d" % s) for s in range(4)]
    B_xs = [Buf("xs0"), Buf("xs1")]
    B_pt = [Buf("pt0"), Buf("pt1")]
    B_actT = [Buf("actT%d" % c) for c in range(NFC)]
    B_A = [Buf("A%d" % s) for s in range(4)]
    B_PT = [Buf("PT%d" % i) for i in range(3)]
    B_u = [Buf("u%d" % i) for i in range(8)]
    B_ring = [Buf("ring%d" % i) for i in range(RING_N)]
    B_negc = [Buf("negc%d" % i) for i in range(NT)]
    B = {k: Buf(k) for k in ["gb", "pT", "hb0", "hb1", "hb2_0", "hb2_1", "t32_0", "t32_1", "t32b", "sg0", "sg1", "yTs0", "yTs1", "ptile", "pb", "cbf", "trif", "e127f",
                             "small", "wpool", "fb0", "fb1", "e10", "e11", "nlf0", "nlf1", "r10", "r11", "r20", "r21", "csp0", "csp1", "rl",
                             "win_b", "wout_b", "wg_b", "wu_b", "wd_b", "wple_b", "wgate_b"]}
    B_st = [Buf("st%d" % i) for i in range(8)]
    B_y = [Buf("y%d" % s) for s in range(4)]

    op = S.op

    def load_xp(I):
        r0 = I * TB
        op("sp", lambda e: e.dma_start(out=xres[:], in_=x_d[r0:r0 + TB, :].rearrange("(s p) d -> p s d", p=128)),
           writes=B_x, dma=True, semkey="x")

    stage = actT[:].rearrange("p c t -> p (c t)").bitcast(F32)
    op("sp", lambda e: e.dma_start(out=stage[:, 0:NCST_MAT * 128], in_=cst_d), writes=B_actT, dma=True, semkey="cst")
    op("sp", lambda e: e.dma_start(out=stage[:, 2048:2560], in_=w_pool_d.rearrange("g c d -> c g d")), writes=B_actT, dma=True, semkey="cst")
    op("sp", lambda e: e.dma_start(out=small[:], in_=small_d), writes=[B["small"]], dma=True, semkey="small")
    op("dve", lambda e: e.tensor_copy(out=cbf[:, 0:2, :].rearrange("p a b -> p (a b)"), in_=stage[:, 0:256]), reads=B_actT, writes=[B["cbf"]])
    op("dve", lambda e: e.tensor_copy(out=cbf[:, 2:14, :].rearrange("p a b -> p (a b)"), in_=stage[:, 3 * 128:15 * 128]), reads=B_actT, writes=[B["cbf"]])
    op("dve", lambda e: e.tensor_copy(out=trif[:], in_=stage[:, 128:256]), reads=B_actT, writes=[B["trif"]])
    op("dve", lambda e: e.tensor_copy(out=e127f[:], in_=stage[:, 256:384]), reads=B_actT, writes=[B["e127f"]])
    op("dve", lambda e: e.tensor_scalar(out=negm[:], in0=stage[:, 128:256], scalar1=-1.0, scalar2=30000.0, op0=ALU.add, op1=ALU.mult), reads=B_actT, writes=[B["cbf"]])
    op("dve", lambda e: e.tensor_tensor(out=wpool_bf[:].rearrange("p g d -> p (g d)"), in0=stage[:, 2048:2560], in1=small[:, SM_PS:SM_PS + 512], op=ALU.mult),
       reads=B_actT + [B["small"]], writes=[B["wpool"]])
    op("dve", lambda e: e.memset(Va[:, :, :, 64:65], 1.0), writes=B_Va)
    op("dve", lambda e: e.memset(KTcur[64:67, :, :], 1.0), writes=B_KTcur)
    op("dve", lambda e: e.memset(u_sb[:, 7, :], 0.0), writes=[B_u[7]])
    def cast_w(src, dst, rows, key):
        step = 512 if rows % 512 == 0 else 256
        for r0 in range(0, rows, step):
            op("pool", lambda e, r0=r0: e.dma_start(out=dst[r0:r0 + step, :], in_=src[r0:r0 + step, :]), writes=[B[key]], dma=True, semkey=key)

    for i_, (c0_, n_) in enumerate([(1024, 512), (1536, 520), (0, 512), (512, 512)]):
        op("pool", lambda e, i_=i_, c0_=c0_, n_=n_: e.dma_start(out=ring[i_][:, 0:8 * n_].rearrange("p (k n) -> p k n", k=8),
                                                              in_=w_in_d[:, c0_:c0_ + n_].rearrange("(k p) n -> p k n", p=128)),
           writes=[B_ring[i_]], dma=True, semkey="ring%dsw" % i_)
    cast_w(w_in_d, win_b, D, "win_b")
    cast_w(w_out_d, wout_b, D, "wout_b")
    cast_w(w_g_d, wg_b, D, "wg_b")
    cast_w(w_u_d, wu_b, D, "wu_b")
    cast_w(w_d_d, wd_b, DFF, "wd_b")
    cast_w(w_ple_d, wple_b, DPLE, "wple_b")
    cast_w(w_gate_d, wgate_b, D, "wgate_b")

    ring_ctr = [0]
    grp_ctr = [0]

    def ring_load(dmas, srcbufs, slot=None):
        if slot is None:
            i = ring_ctr[0] % RING_N
            ring_ctr[0] += 1
        else:
            i = slot
        slot = ring[i]
        grp_ctr[0] += 1
        for f in dmas:
            o_ap, i_ap = f(slot)
            for j in range(o_ap.shape[1]):
                op("sp", lambda e, o_ap=o_ap, i_ap=i_ap, j=j: e.dma_start(out=o_ap[:, j, :], in_=i_ap[:, j, :]), reads=srcbufs, writes=[B_ring[i]], dma=True,
                   semkey="ring%d" % i, grp=("ring", grp_ctr[0]))
        return slot, B_ring[i]

    def w_kc(src, c0, n):
        return src[:, c0:c0 + n].rearrange("(k p) n -> p k n", p=128)

    st_ctr = [0]

    def gb_load(which):
        op("sp", lambda e: e.dma_start(out=gb[:], in_=gbc_d[:, which * D:(which + 1) * D]), writes=[B["gb"]], dma=True, semkey="gb")

    def rstd_of(src_ap, src_bufs, n, junk_ap, junk_bufs):
        i = st_ctr[0] % 8
        st_ctr[0] += 1
        sc = float(1.0 / np.sqrt(n))
        op("act", lambda e: e.activation(out=junk_ap, in_=src_ap, func=AF.Square, scale=sc, accum_out=st_ms[:, i:i + 1]),
           reads=src_bufs, writes=junk_bufs + [B_st[i]])
        op("act", lambda e: e.activation(out=st_ln[:, i:i + 1], in_=st_ms[:, i:i + 1], func=AF.Ln, bias=EPS), reads=[B_st[i]], writes=[B_st[i]])
        op("act", lambda e: e.activation(out=st_r[:, i:i + 1], in_=st_ln[:, i:i + 1], func=AF.Exp, scale=-0.5), reads=[B_st[i]], writes=[B_st[i]])
        return st_r[:, i:i + 1], B_st[i]

    def transpose_to_hT(src_bf, src_bufs, nk, dst3, dst_bufs, gcol=None, TBANK=7):
        psT = ps[:, TBANK, :].bitcast(BF16)
        for kc in range(nk):
            op("pe", lambda e, kc=kc: e.transpose(out=psT[:, kc * 128:(kc + 1) * 128], in_=src_bf[:, kc * 128:(kc + 1) * 128], identity=cbf[:, 0, :]),
               reads=src_bufs + [B["cbf"]], writes=[bk[TBANK]])
        src3 = psT[:, 0:nk * 128].rearrange("p (k t) -> p k t", k=nk)
        if gcol is None:
            op("dve", lambda e: e.tensor_copy(out=dst3, in_=src3), reads=[bk[TBANK]], writes=dst_bufs)
        else:
            g3 = small[:, gcol:gcol + nk].unsqueeze(2).broadcast_to([128, nk, 128])
            op("dve", lambda e: e.tensor_tensor(out=dst3, in0=src3, in1=g3, op=ALU.mult), reads=[bk[TBANK], B["small"]], writes=dst_bufs)

    rot = [0]

    def nbank(lo, hi):
        b = lo + rot[0] % (hi - lo)
        rot[0] += 1
        return b

    pre_done = set()

    def P1_pre(I):
        ring_load([lambda s: (s[:, 0:4096].rearrange("p (k n) -> p k n", k=8), w_kc(win_b, 1024, 512))], [B["win_b"]], slot=0)
        for c in range(2):
            T = I * 4 + c
            op("sp", lambda e, T=T, c=c: e.dma_start(out=xs[c][:], in_=x_d[T * 128:(T + 1) * 128, :]), writes=[B_xs[c]], dma=True, semkey="xs%d" % c)
        pre_done.add(I)

    def P1c(I, c, wv, wfu, bv, bfu):
        X, Y = 4 + 2 * c, 5 + 2 * c
        hb, fb, e1, nlf, r1, r2, csp = hbA[c], fbA[c], e1A[c], nlfA[c], r1A[c], r2A[c], cspA[c]
        Bhb, Bfb, Be1, Bnlf, Br1, Br2, Bcsp = B["hb%d" % c], B["fb%d" % c], B["e1%d" % c], B["nlf%d" % c], B["r1%d" % c], B["r2%d" % c], B["csp%d" % c]
        for s in (c, c + 2):
            T = I * 4 + s
            xi = c
            if not (I in pre_done and s == c):
                op("sp", lambda e, T=T, xi=xi: e.dma_start(out=xs[xi][:], in_=x_d[T * 128:(T + 1) * 128, :]), writes=[B_xs[xi]], dma=True, semkey="xs%d" % xi)
            r, rb = rstd_of(xs[xi][:], [B_xs[xi]], D, hb[:], [Bhb])
            op("dve", lambda e, xi=xi, r=r: e.tensor_scalar(out=hb[:], in0=xs[xi][:], scalar1=r, scalar2=None, op0=ALU.mult), reads=[B_xs[xi], rb], writes=[Bhb])
            yield 1
            transpose_to_hT(hb, [Bhb], 8, hT[:, :, s * 128:(s + 1) * 128], [B_hT[s]] + B_actT[0:8], gcol=SM_GPRE, TBANK=X)
            yield 1
            for kc in range(8):
                op("pe", lambda e, kc=kc, s=s: e.matmul(ps[:, X, :], lhsT=hT[:, kc, s * 128:(s + 1) * 128], rhs=wv[:, kc, :], start=(kc == 0), stop=(kc == 7)),
                   reads=[B_hT[s], bv], writes=[bk[X]])
            op("act", lambda e, T=T: e.copy(out=Va[:, T, :, 0:64], in_=ps[:, X, :].rearrange("p (h d) -> p h d", h=NH)), reads=[bk[X]], writes=[B_Va[T]])
            for kc in range(8):
                op("pe", lambda e, kc=kc, s=s: e.matmul(ps[:, Y, :], lhsT=hT[:, kc, s * 128:(s + 1) * 128], rhs=wfu[:, kc, 8:520], start=(kc == 0), stop=(kc == 7)),
                   reads=[B_hT[s], bfu], writes=[bk[Y]])
            op("dve", lambda e, T=T: e.tensor_copy(out=u_sb[:, T % 8, :], in_=ps[:, Y, :]), reads=[bk[Y]], writes=[B_u[T % 8]])
            yield 1
            for kc in range(8):
                op("pe", lambda e, kc=kc, s=s: e.matmul(ps[:, X, 0:8], lhsT=hT[:, kc, s * 128:(s + 1) * 128], rhs=wfu[:, kc, 0:8], start=(kc == 0), stop=(kc == 7)),
                   reads=[B_hT[s], bfu], writes=[bk[X]])
            op("dve", lambda e: e.tensor_tensor(out=fb[:], in0=ps[:, X, 0:8], in1=small[:, SM_B:SM_B + 8], op=ALU.add), reads=[bk[X], B["small"]], writes=[Bfb])
            op("act", lambda e: e.activation(out=e1[:], in_=fb[:], func=AF.Exp, scale=-1.0), reads=[Bfb], writes=[Be1])
            op("act", lambda e: e.activation(out=nlf[:], in_=e1[:], func=AF.Ln, bias=1.0), reads=[Be1], writes=[Bnlf])
            yield 1
            op("pe", lambda e, T=T: e.matmul(ps[:, X, 8:16], lhsT=trif[:], rhs=nlf[:], start=True, stop=(T == 0)), reads=[B["trif"], Bnlf], writes=[bk[X]])
            if T > 0:
                op("pe", lambda e, T=T: e.matmul(ps[:, X, 8:16], lhsT=e127f[:], rhs=negc[:, T - 1, :], start=False, stop=True),
                   reads=[B["e127f"], B_negc[T - 1]], writes=[bk[X]])
            op("dve", lambda e, T=T: e.tensor_copy(out=negc[:, T, :], in_=ps[:, X, 8:16]), reads=[bk[X]], writes=[B_negc[T]])
            yield 1
            op("dve", lambda e, T=T: e.tensor_scalar(out=csp[:, :, 0], in0=negc[:, T, :], scalar1=-1.0, scalar2=None, op0=ALU.mult), reads=[B_negc[T]], writes=[Bcsp])
            op("dve", lambda e, T=T: e.tensor_tensor(out=r1[:], in0=negc[:, T, :], in1=csp[:, :, 0], op=ALU.add), reads=[B_negc[T], Bcsp], writes=[Br1])
            op("dve", lambda e: e.tensor_scalar(out=csp[:, :, 1], in0=r1[:], scalar1=-1.0, scalar2=None, op0=ALU.mult), reads=[Br1], writes=[Bcsp])
            op("dve", lambda e: e.tensor_tensor(out=r2[:], in0=r1[:], in1=csp[:, :, 1], op=ALU.add), reads=[Br1, Bcsp], writes=[Br2])
            op("dve", lambda e: e.tensor_scalar(out=csp[:, :, 2], in0=r2[:], scalar1=-1.0, scalar2=None, op0=ALU.mult), reads=[Br2], writes=[Bcsp])
            yield 1
            for hg in range(2):
                cb = Y if hg == 0 else X
                for hh in range(4):
                    h = hg * 4 + hh
                    op("pe", lambda e, h=h, hh=hh, cb=cb: e.matmul(ps[64:67, cb, hh * 128:(hh + 1) * 128], lhsT=csp[:, h, :], rhs=cbf[:, 0, :], start=True, stop=True),
                       reads=[Bcsp, B["cbf"]], writes=[bk[cb]])
                op("act", lambda e, hg=hg, cb=cb, s=s: e.copy(out=QTa[64:67, hg * 4:hg * 4 + 4, s * 128:(s + 1) * 128],
                                                             in_=ps[64:67, cb, :].rearrange("p (h t) -> p h t", h=4)),
                   reads=[bk[cb]], writes=[B_QTa[hg * 4 + i] for i in range(4)])
            yield 1

    def P1(I):
        if I not in pre_done and I > 0:
            ring_load([lambda s: (s[:, 0:4096].rearrange("p (k n) -> p k n", k=8), w_kc(win_b, 1024, 512))], [B["win_b"]], slot=0)
        slot_v, bv = ring[0], B_ring[0]
        if I > 0:
            ring_load([lambda s: (s[:, 0:8 * 520].rearrange("p (k n) -> p k n", k=8), w_kc(win_b, 1536, 520))], [B["win_b"]], slot=1)
        slot_fu, bfu = ring[1], B_ring[1]
        wv = slot_v[:, 0:4096].rearrange("p (k n) -> p k n", k=8)
        wfu = slot_fu[:, 0:8 * 520].rearrange("p (k n) -> p k n", k=8)
        yield from merge_gen([P1c(I, 0, wv, wfu, bv, bfu), P1c(I, 1, wv, wfu, bv, bfu)], [14, 14])
        qs, ks = (0, 1) if I > 0 else (2, 3)
        if I > 0:
            ring_load([lambda s_: (s_[:, 0:4096].rearrange("p (k n) -> p k n", k=8), w_kc(win_b, 0, 512))], [B["win_b"]], slot=0)
            ring_load([lambda s_: (s_[:, 0:4096].rearrange("p (k n) -> p k n", k=8), w_kc(win_b, 512, 512))], [B["win_b"]], slot=1)
        wq = ring[qs][:, 0:4096].rearrange("p (k n) -> p k n", k=8)
        wk = ring[ks][:, 0:4096].rearrange("p (k n) -> p k n", k=8)
        bq, bkk = B_ring[qs], B_ring[ks]
        for j in range(4):
            qb = 4 + (2 * j) % 4
            for kc in range(8):
                op("pe", lambda e, kc=kc, j=j, qb=qb: e.matmul(ps[:, qb, :], lhsT=wq[:, kc, j * 128:(j + 1) * 128], rhs=hT[:, kc, :], start=(kc == 0), stop=(kc == 7)),
                   reads=B_hT + [bq], writes=[bk[qb]])
            op("act", lambda e, j=j, qb=qb: e.activation(out=QTa[0:64, 2 * j, :], in_=ps[0:64, qb, :], func=AF.Copy, scale=0.125), reads=[bk[qb]], writes=[B_QTa[2 * j]])
            op("act", lambda e, j=j, qb=qb: e.activation(out=QTa[0:64, 2 * j + 1, :], in_=ps[64:128, qb, :], func=AF.Copy, scale=0.125), reads=[bk[qb]], writes=[B_QTa[2 * j + 1]])
            yield 1
            kb = 4 + (2 * j + 1) % 4
            for kc in range(8):
                op("pe", lambda e, kc=kc, j=j, kb=kb: e.matmul(ps[:, kb, :], lhsT=wk[:, kc, j * 128:(j + 1) * 128], rhs=hT[:, kc, :], start=(kc == 0), stop=(kc == 7)),
                   reads=B_hT + [bkk], writes=[bk[kb]])
            op("dve", lambda e, j=j, kb=kb: e.tensor_copy(out=KTcur[0:64, 2 * j, :], in_=ps[0:64, kb, :]), reads=[bk[kb]], writes=[B_KTcur[2 * j]])
            op("dve", lambda e, j=j, kb=kb: e.tensor_copy(out=KTcur[0:64, 2 * j + 1, :], in_=ps[64:128, kb, :]), reads=[bk[kb]], writes=[B_KTcur[2 * j + 1]])
            yield 1
        if I < NBLK - 1:
            grp_ctr[0] += 1
            for h_ in range(NH):
                op("pool", lambda e, I=I, h_=h_: e.dma_start(out=kt_d[h_, :, I * TB:(I + 1) * TB], in_=KTcur[0:67, h_, :]),
                   reads=B_KTcur, writes=[B_ktd[I]], dma=True, semkey="ktw", grp=("ktw", grp_ctr[0]))

    att_ctr = [0]
    kts_done = set()

    def kts_issue(I, h):
        if I == 0 or (I, h) in kts_done:
            return
        kts_done.add((I, h))
        kb_ = h % 2
        grp_ctr[0] += 1
        for c0 in range(0, I * TB, 1024):
            c1 = min(c0 + 1024, I * TB)
            op("pool", lambda e, c0=c0, c1=c1: e.dma_start(out=KTs[kb_][0:67, c0:c1], in_=kt_d[h, :, c0:c1]),
               reads=B_ktd[0:I], writes=[B_KTs[kb_]], dma=True, semkey="kts%d" % kb_, grp=("kts", grp_ctr[0]))


    def att_gen(I):
        nprev = 4 * I
        items = [(h, j) for h in range(NH) for j in range(nprev + 4)]

        def emit_S(h, j, idx):
            jj = j - nprev
            q_lo = max(0, jj) * 128
            sbk = idx % 3
            pti = idx % 3
            kb_ = h % 2
            if j < nprev:
                lhs = KTs[kb_][0:67, j * 128:(j + 1) * 128]
                lb = [B_KTs[kb_]]
            else:
                lhs = KTcur[0:67, h, jj * 128:(jj + 1) * 128]
                lb = [B_KTcur[h]]
            op("pe", lambda e: e.matmul(ps[:, sbk, q_lo:TB], lhsT=lhs, rhs=QTa[0:67, h, q_lo:TB], start=True, stop=(jj < 0)),
               reads=lb + [B_QTa[h]], writes=[bk[sbk]])
            if jj >= 0:
                op("pe", lambda e: e.matmul(ps[:, sbk, q_lo:q_lo + 128], lhsT=cbf[:, 0, :], rhs=negm[:], start=False, stop=True),
                   reads=[B["cbf"]], writes=[bk[sbk]])
            op("act", lambda e: e.activation(out=PT[pti][:, q_lo:TB], in_=ps[:, sbk, q_lo:TB], func=AF.Exp, bias=negc[:, j, h:h + 1], scale=1.0),
               reads=[bk[sbk], B_negc[j]], writes=[B_PT[pti]])

        def emit_PV(h, j, idx):
            jj = j - nprev
            pti = idx % 3
            ob = 3
            Ov = ps[:, ob, 0:260].rearrange("p (s c) -> p s c", s=4)
            for ii in range(max(0, jj), 4):
                op("pe", lambda e, ii=ii: e.matmul(Ov[:, ii, :], lhsT=PT[pti][:, ii * 128:(ii + 1) * 128], rhs=Va[:, j, h, :],
                                                  start=(j == 0 and ii == 0), stop=(j == nprev + 3 and ii == 3)),
                   reads=[B_PT[pti], B_Va[j]], writes=[bk[ob]])
            if j == nprev + 3:
                op("dve", lambda e: e.reciprocal(out=rl[:], in_=Ov[:, :, 64]), reads=[bk[ob]], writes=[B["rl"]])
                op("dve", lambda e: e.tensor_tensor(out=A_sb[:, :, h * 64:(h + 1) * 64], in0=Ov[:, :, 0:64],
                                                    in1=rl[:, :].unsqueeze(2).broadcast_to([128, 4, 64]), op=ALU.mult),
                   reads=[bk[ob], B["rl"]], writes=B_A)

        def kts_load(h):
            kts_issue(I, h)

        base = att_ctr[0]
        att_ctr[0] += len(items)
        kts_load(0)
        for n, (h, j) in enumerate(items):
            if j == 0 and h + 1 < NH:
                kts_load(h + 1)
            if n == 0:
                emit_S(h, j, base + n)
                if len(items) > 1:
                    emit_S(items[1][0], items[1][1], base + 1)
            if n + 2 < len(items):
                h2, j2 = items[n + 2]
                emit_S(h2, j2, base + n + 2)
            emit_PV(h, j, base + n)
            yield 0.55

    def P3ac(I, c, wo, bo):
        X, Y = 2 * c, 2 * c + 1
        hb2, t32c, yTs = hb2A[c], t32A[c], yTsA[c]
        Bhb2, Bt32, ByTs = B["hb2_%d" % c], B["t32_%d" % c], B["yTs%d" % c]
        for s in (c, c + 2):
            T = I * 4 + s
            for g in range(4):
                bt = (C_BT0 if T == 0 else C_BT) + g - 1
                bp = C_BP + g - 1
                op("pe", lambda e, g=g, T=T, bt=bt: e.matmul(ps[:, X, g * 128:(g + 1) * 128], lhsT=u_sb[:, T % 8, g * 128:(g + 1) * 128], rhs=cbf[:, bt, :], start=True, stop=False),
                   reads=[B_u[T % 8], B["cbf"]], writes=[bk[X]])
                op("pe", lambda e, g=g, T=T, bp=bp: e.matmul(ps[:, X, g * 128:(g + 1) * 128], lhsT=u_sb[:, (T - 1) % 8, g * 128:(g + 1) * 128], rhs=cbf[:, bp, :], start=False, stop=True),
                   reads=[B_u[(T - 1) % 8], B["cbf"]], writes=[bk[X]])
            op("act", lambda e: e.copy(out=yTs[:], in_=ps[:, X, :]), reads=[bk[X]], writes=[ByTs])
            yield 1
            for g in range(4):
                op("pe", lambda e, g=g: e.matmul(ps[:, Y, g * 128:(g + 1) * 128], lhsT=yTs[:, g * 128:(g + 1) * 128], rhs=wpool_bf[:, g, :], start=True, stop=True),
                   reads=[ByTs, B["wpool"]], writes=[bk[Y]])
            rA, rAb = rstd_of(A_sb[:, s, :], [B_A[s]], DATT, hb2[:, 0:512], [Bhb2])
            op("act", lambda e, s=s, rA=rA: e.activation(out=hb2[:, 0:512], in_=A_sb[:, s, :], func=AF.Copy, scale=rA), reads=[B_A[s], rAb], writes=[Bhb2])
            yield 1
            rB, rBb = rstd_of(ps[:, Y, :], [bk[Y]], DPOOL, hb2[:, 512:1024], [Bhb2])
            op("act", lambda e, rB=rB: e.activation(out=hb2[:, 512:1024], in_=ps[:, Y, :], func=AF.Copy, scale=rB), reads=[bk[Y], rBb], writes=[Bhb2])
            yield 1
            transpose_to_hT(hb2, [Bhb2], 8, hT2[:, :, s * 128:(s + 1) * 128], [B_hT2[s]], gcol=SM_GMIX, TBANK=X)
            yield 1
            for half in range(2):
                for kc in range(8):
                    op("pe", lambda e, kc=kc, s=s, half=half: e.matmul(ps[:, X + half, :], lhsT=hT2[:, kc, s * 128:(s + 1) * 128], rhs=wo[half][:, kc, :],
                                                                    start=(kc == 0), stop=(kc == 7)),
                       reads=[B_hT2[s], bo[half]], writes=[bk[X + half]])
            pair = ps[:, X:X + 2, :].rearrange("p a n -> p (a n)")
            yield 1
            r, rb = rstd_of(pair, [bk[X], bk[Y]], D, t32c[:], [Bt32])
            yield 1
            op("dve", lambda e, pair=pair, r=r: e.scalar_tensor_tensor(out=t32c[:], in0=pair, scalar=r, in1=gb[:], op0=ALU.mult, op1=ALU.mult),
               reads=[bk[X], bk[Y], rb, B["gb"]], writes=[Bt32])
            op("dve", lambda e, s=s: e.tensor_tensor(out=xres[:, s, :], in0=xres[:, s, :], in1=t32c[:], op=ALU.add), reads=[B_x[s], Bt32], writes=[B_x[s]])
            yield 1
            r, rb = rstd_of(xres[:, s, :], [B_x[s]], D, hb2[:], [Bhb2])
            op("dve", lambda e, s=s, r=r: e.tensor_scalar(out=hb2[:], in0=xres[:, s, :], scalar1=r, scalar2=None, op0=ALU.mult), reads=[B_x[s], rb], writes=[Bhb2])
            yield 1
            transpose_to_hT(hb2, [Bhb2], 8, hT2[:, :, s * 128:(s + 1) * 128], [B_hT2[s]], gcol=SM_GFFN, TBANK=X)
            yield 1

    def P3a(I):
        if I == 0:
            load_xp(I)
        slot_o0, bo0 = ring_load([lambda s: (s[:, 0:4096].rearrange("p (k n) -> p k n", k=8), w_kc(wout_b, 0, 512))], [B["wout_b"]], slot=2)
        slot_o1, bo1 = ring_load([lambda s: (s[:, 0:4096].rearrange("p (k n) -> p k n", k=8), w_kc(wout_b, 512, 512))], [B["wout_b"]], slot=3)
        wo = [slot_o0[:, 0:4096].rearrange("p (k n) -> p k n", k=8), slot_o1[:, 0:4096].rearrange("p (k n) -> p k n", k=8)]
        bo = [bo0, bo1]
        yield from merge_gen([P3ac(I, 0, wo, bo), P3ac(I, 1, wo, bo)], [18, 18])

    def P3bcd_gen(I):
        for pc in range(NFC // 2):
            c0 = pc * 256
            if pc == 2:
                gb_load(1)
            slot_gu, bgu = ring_load([lambda s, c0=c0: (s[:, 0:2048].rearrange("p (k n) -> p k n", k=8), w_kc(wg_b, c0, 256)),
                                      lambda s, c0=c0: (s[:, 2048:4096].rearrange("p (k n) -> p k n", k=8), w_kc(wu_b, c0, 256))], [B["wg_b"], B["wu_b"]])
            wgv = slot_gu[:, 0:2048].rearrange("p (k n) -> p k n", k=8)
            wuv = slot_gu[:, 2048:4096].rearrange("p (k n) -> p k n", k=8)
            for m in range(2):
                c = pc * 2 + m
                gbk = 4 + 2 * (c % 2)
                for kc in range(8):
                    op("pe", lambda e, kc=kc, m=m, gbk=gbk, wgv=wgv: e.matmul(ps[:, gbk, :], lhsT=wgv[:, kc, m * 128:(m + 1) * 128], rhs=hT2[:, kc, :], start=(kc == 0), stop=(kc == 7)),
                       reads=B_hT2 + [bgu], writes=[bk[gbk]])
                for kc in range(8):
                    op("pe", lambda e, kc=kc, m=m, gbk=gbk, wuv=wuv: e.matmul(ps[:, gbk + 1, :], lhsT=wuv[:, kc, m * 128:(m + 1) * 128], rhs=hT2[:, kc, :], start=(kc == 0), stop=(kc == 7)),
                       reads=B_hT2 + [bgu], writes=[bk[gbk + 1]])
                si = c % 2
                op("act", lambda e, gbk=gbk, si=si: e.activation(out=sg32[si], in_=ps[:, gbk, :], func=AF.Exp, scale=-1.0), reads=[bk[gbk]], writes=[B["sg%d" % si]])
                op("act", lambda e, si=si: e.activation(out=sg32[si], in_=sg32[si], func=AF.Ln, bias=1.0), reads=[B["sg%d" % si]], writes=[B["sg%d" % si]])
                op("act", lambda e, si=si: e.activation(out=sg32[si], in_=sg32[si], func=AF.Exp, scale=-1.0), reads=[B["sg%d" % si]], writes=[B["sg%d" % si]])
                op("dve", lambda e, gbk=gbk, si=si: e.tensor_tensor(out=sg32[si], in0=sg32[si], in1=ps[:, gbk, :], op=ALU.mult),
                   reads=[B["sg%d" % si], bk[gbk]], writes=[B["sg%d" % si]])
                op("dve", lambda e, gbk=gbk, si=si, c=c: e.tensor_tensor(out=actT[:, c, :], in0=sg32[si], in1=ps[:, gbk + 1, :], op=ALU.mult),
                   reads=[B["sg%d" % si], bk[gbk + 1]], writes=[B_actT[c]] + (B_hT if c < 8 else []))
                yield 3.5
        def p_load(s_):
            T_ = I * 4 + s_
            op("sp", lambda e: e.dma_start(out=ptile[:, T_ % 2, :], in_=p_d[T_ * 128:(T_ + 1) * 128, :]), writes=[B_pt[T_ % 2]], dma=True, semkey="p%d" % (T_ % 2))

        for pr in range(2):
            if pr == 1:
                p_load(0)
                p_load(1)
            for pc in range(6):
                nch = 4 if pc < 5 else 2
                slot_d, bd = ring_load([lambda s, pc=pc, nch=nch: (s[:, 0:nch * 1024].rearrange("p (c n) -> p c n", c=nch),
                                                                  wd_b[pc * 512:pc * 512 + nch * 128, :].rearrange("(c p) n -> p c n", p=128))], [B["wd_b"]])
                wdv = slot_d[:, 0:nch * 1024].rearrange("p (c n) -> p c n", c=nch)
                for cc in range(nch):
                    c = pc * 4 + cc
                    for si in range(2):
                        s = pr * 2 + si
                        for half in range(2):
                            op("pe", lambda e, c=c, cc=cc, s=s, si=si, half=half, wdv=wdv: e.matmul(ps[:, 4 + si * 2 + half, :], lhsT=actT[:, c, s * 128:(s + 1) * 128],
                                                                                                 rhs=wdv[:, cc, half * 512:(half + 1) * 512], start=(c == 0), stop=(c == NFC - 1)),
                               reads=[B_actT[c], bd], writes=[bk[4 + si * 2 + half]])
                    yield 1.6
            if pr == 1:
                slot_p, bp_ = ring_load([lambda s: (s[:, 0:2048].rearrange("p (k n) -> p k n", k=2), wple_b.rearrange("(k p) n -> p k n", p=128))], [B["wple_b"]], slot=1)
                slot_g0, bg0 = ring_load([lambda s: (s[:, 0:4096].rearrange("p (k n) -> p k n", k=8), w_kc(wgate_b, 0, 512))], [B["wgate_b"]], slot=2)
                slot_g1, bg1 = ring_load([lambda s: (s[:, 0:4096].rearrange("p (k n) -> p k n", k=8), w_kc(wgate_b, 512, 512))], [B["wgate_b"]], slot=3)
            for si in range(2):
                s = pr * 2 + si
                b0 = 4 + si * 2
                pair = ps[:, b0:b0 + 2, :].rearrange("p a n -> p (a n)")
                r, rb = rstd_of(pair, [bk[b0], bk[b0 + 1]], D, t32[:], [B["t32_0"]])
                op("dve", lambda e, pair=pair, r=r: e.scalar_tensor_tensor(out=t32[:], in0=pair, scalar=r, in1=gb[:], op0=ALU.mult, op1=ALU.mult),
                   reads=[bk[b0], bk[b0 + 1], rb, B["gb"]], writes=[B["t32_0"]])
                op("dve", lambda e, s=s: e.tensor_tensor(out=xres[:, s, :], in0=xres[:, s, :], in1=t32[:], op=ALU.add), reads=[B_x[s], B["t32_0"]], writes=[B_x[s]])
                yield 4.0
            if pr == 1:
                gb_load(2)

        wpl = slot_p[:, 0:2048].rearrange("p (k n) -> p k n", k=2)
        wgt = [slot_g0[:, 0:4096].rearrange("p (k n) -> p k n", k=8), slot_g1[:, 0:4096].rearrange("p (k n) -> p k n", k=8)]
        bgt = [bg0, bg1]
        for s in range(4):
            T = I * 4 + s
            pi = T % 2
            op("act", lambda e, pi=pi: e.copy(out=pb[:], in_=ptile[:, pi, :]), reads=[B_pt[pi]], writes=[B["pb"]])
            if s < 2:
                p_load(s + 2)
            transpose_to_hT(pb, [B["pb"]], 2, pT[:, :, :], [B["pT"]])
            for half in range(2):
                for kc in range(2):
                    op("pe", lambda e, kc=kc, half=half: e.matmul(ps[:, 4 + half, :], lhsT=pT[:, kc, :], rhs=wpl[:, kc, half * 512:(half + 1) * 512], start=(kc == 0), stop=(kc == 1)),
                       reads=[B["pT"], bp_], writes=[bk[4 + half]])
            pairE = ps[:, 4:6, :].rearrange("p a n -> p (a n)")
            rE, rEb = rstd_of(pairE, [bk[4], bk[5]], D, t32b[:], [B["t32b"], B["sg0"], B["sg1"]])
            op("dve", lambda e, pairE=pairE, rE=rE: e.scalar_tensor_tensor(out=t32b[:], in0=pairE, scalar=rE, in1=gb[:], op0=ALU.mult, op1=ALU.mult),
               reads=[bk[4], bk[5], rEb, B["gb"]], writes=[B["t32b"], B["sg0"], B["sg1"]])
            if s == 0 and I + 2 < NBLK:
                P1_pre(I + 2)
            yield 5.0
            op("dve", lambda e, s=s: e.tensor_copy(out=hb2A[0][:], in_=xres[:, s, :]), reads=[B_x[s]], writes=[B["hb2_0"]])
            transpose_to_hT(hb2A[0], [B["hb2_0"]], 8, hT2[:, :, s * 128:(s + 1) * 128], [B_hT2[s]])
            for half in range(2):
                for kc in range(8):
                    op("pe", lambda e, kc=kc, s=s, half=half: e.matmul(ps[:, 6 + half, :], lhsT=hT2[:, kc, s * 128:(s + 1) * 128], rhs=wgt[half][:, kc, :],
                                                                    start=(kc == 0), stop=(kc == 7)),
                       reads=[B_hT2[s], bgt[half]], writes=[bk[6 + half]])
            pairG = ps[:, 6:8, :].rearrange("p a n -> p (a n)")
            op("act", lambda e, pairG=pairG: e.activation(out=t32[:], in_=pairG, func=AF.Exp, scale=-1.0), reads=[bk[6], bk[7]], writes=[B["t32_0"]])
            op("act", lambda e: e.activation(out=t32[:], in_=t32[:], func=AF.Ln, bias=1.0), reads=[B["t32_0"]], writes=[B["t32_0"]])
            op("act", lambda e: e.activation(out=t32[:], in_=t32[:], func=AF.Exp, scale=-1.0), reads=[B["t32_0"]], writes=[B["t32_0"]])
            op("dve", lambda e: e.tensor_tensor(out=t32[:], in0=t32[:], in1=t32b[:], op=ALU.mult), reads=[B["t32_0"], B["t32b"], B["sg0"], B["sg1"]], writes=[B["t32_0"]])
            op("dve", lambda e, s=s: e.tensor_tensor(out=xres[:, s, :], in0=xres[:, s, :], in1=t32[:], op=ALU.add), reads=[B_x[s], B["t32_0"]], writes=[B_x[s]])
            op("pool", lambda e, s=s, T=T: e.dma_start(out=y_d[T * 128:(T + 1) * 128, :], in_=xres[:, s, :]), reads=[B_x[s]], writes=[B_y[s]], dma=True, semkey="y%d" % s)
            if I + 1 < NBLK:
                op("pool", lambda e, s=s, T=T: e.dma_start(out=xres[:, s, :], in_=x_d[(T + 4) * 128:(T + 5) * 128, :]), writes=[B_x[s]], dma=True, semkey="x%d" % s)
            if s == 3 and I + 1 < NBLK:
                gb_load(0)
            yield 14.0

    def merge_gen(gens, tots):
        t = [0.0] * len(gens)
        live = [g is not None for g in gens]
        while any(live):
            k = min((i for i in range(len(gens)) if live[i]), key=lambda i: (t[i] / tots[i], i))
            try:
                c = next(gens[k])
                t[k] += c
                yield c
            except StopIteration:
                live[k] = False

    def merge(ga, ta_tot, gb, tb_tot):
        ta = tb = 0.0
        a_done = ga is None
        b_done = gb is None
        while not (a_done and b_done):
            if (not a_done) and (b_done or ta / ta_tot <= tb / tb_tot):
                try:
                    ta += next(ga)
                except StopIteration:
                    a_done = True
            else:
                try:
                    tb += next(gb)
                except StopIteration:
                    b_done = True

    TB_COST = 11 * 2 * 3.5 + 2 * (22 * 1.6 + 8.0) + 4 * 19.0
    NY1 = 28 + 8
    NY3 = 26
    gb_load(0)
    for _ in P1(0):
        pass
    merge(att_gen(0), 1.0, None, 1.0)
    for I in range(NBLK):
        if I + 1 < NBLK:
            kts_issue(I + 1, 0)
            kts_issue(I + 1, 1)
            merge(P1(I + 1), NY1, P3a(I), NY3)
            merge(att_gen(I + 1), NH * (4 * (I + 1) + 4) * 0.55, P3bcd_gen(I), TB_COST)
        else:
            merge(P3a(I), 1.0, None, 1.0)
            merge(None, 1.0, P3bcd_gen(I), TB_COST)

    S.emit()
    return nc


def make_maps(inputs, S_LEN, n_cores):
    f = lambda a: np.ascontiguousarray(np.asarray(a, dtype=np.float32))
    cst = host_consts()
    small = np.zeros((128, NSMALL), np.float32)
    small[:, SM_B:SM_B + 8] = np.broadcast_to(f(inputs["b_forget"])[0][None, :], (128, 8))
    small[:, SM_PS:SM_PS + 512] = np.broadcast_to(f(inputs["pool_scale"])[0][None, :], (128, 512))
    small[:, SM_GPRE:SM_GPRE + 8] = f(inputs["g_mix_pre"])[0].reshape(8, 128).T
    gmix = np.concatenate([f(inputs["g_attn_grp"])[0], f(inputs["g_pool_grp"])[0]])
    small[:, SM_GMIX:SM_GMIX + 8] = gmix.reshape(8, 128).T
    small[:, SM_GFFN:SM_GFFN + 8] = f(inputs["g_ffn_pre"])[0].reshape(8, 128).T
    gbc = np.concatenate([f(inputs["g_mix_post"])[0], f(inputs["g_ffn_post"])[0], f(inputs["g_ple"])[0]])
    gbc = np.ascontiguousarray(np.broadcast_to(gbc[None, :], (128, 3 * D)))
    shared = {
        "w_in": f(inputs["w_in"])[0], "w_pool": f(inputs["w_pool"])[0], "w_out": f(inputs["w_out"])[0],
        "w_ffn_gate": f(inputs["w_ffn_gate"])[0], "w_ffn_up": f(inputs["w_ffn_up"])[0], "w_ffn_down": f(inputs["w_ffn_down"])[0],
        "w_ple_proj": f(inputs["w_ple_proj"])[0], "w_ple_gate": f(inputs["w_ple_gate"])[0],
        "cst": cst, "small": small, "gbc": gbc,
    }
    x = f(inputs["x"])
    p = f(inputs["p"])[0]
    maps = []
    for c in range(n_cores):
        m = dict(shared)
        m["x"] = np.ascontiguousarray(x[c, :S_LEN])
        m["p"] = np.ascontiguousarray(p[c, :S_LEN])
        maps.append(m)
    return maps


_NC_CACHE = {}


def kernel(**inputs):
    S_LEN = 4096
    n = 8
    if S_LEN not in _NC_CACHE:
        _NC_CACHE[S_LEN] = build_nc(S_LEN)
    nc = _NC_CACHE[S_LEN]
    maps = make_maps(inputs, S_LEN, n)
    res = run_bass_kernel_spmd(nc, maps, core_ids=list(range(n)))
    return np.stack([np.asarray(r["y"], dtype=np.float32) for r in res.results], axis=0)
```
